# Optimizing a Trainium2 kernel written in Bass

```python
import jax, jax.numpy as jnp
from jax import lax
import numpy as np

D_MODEL = 2048
BATCH = 2
SEQ = 16384
DEPTH = 1

GRID_W = 64
CTX_LEN = 256
NA_HEADS = 16
NA_HEAD_DIM = 64
NA_WIDTH = NA_HEADS * NA_HEAD_DIM
WIN_H = 8
WIN_W = 16
GLA_HEADS = 4
GLA_KEY_DIM = D_MODEL // 2
GLA_VAL_DIM = D_MODEL
GLA_HEAD_K = GLA_KEY_DIM // GLA_HEADS
GLA_HEAD_V = GLA_VAL_DIM // GLA_HEADS
GATE_RANK = 16
GATE_NORMALIZER = 16.0
GLA_CHUNK = 64
N_DIRS = 2
ROPE_BASE = 10000.0
ROPE_PAIRS = GLA_HEAD_K // 4
GATE_WIDTH = GLA_HEADS * 2 * ROPE_PAIRS
COLS = (NA_WIDTH, NA_WIDTH, NA_WIDTH, NA_WIDTH,
        GLA_KEY_DIM, GLA_KEY_DIM, GLA_VAL_DIM, GLA_VAL_DIM, N_DIRS * GATE_RANK,
        D_MODEL, D_MODEL)
PROJ_WIDTH = sum(COLS)
DEEPNORM_ALPHA = (2 * DEPTH) ** 0.25
DEEPNORM_BETA = (8 * DEPTH) ** -0.25
LN_EPS = 1e-6
RMS_EPS = 1e-6

kernel_name = "hybrid_natten_gla_gated_merge_block"


def _layer_norm(x, g=None, b=None):
    xf = x.astype(jnp.float32)
    mu = jnp.mean(xf, axis=-1, keepdims=True)
    var = jnp.mean(jnp.square(xf - mu), axis=-1, keepdims=True)
    y = (xf - mu) * lax.rsqrt(var + LN_EPS)
    if g is not None:
        y = y * g.astype(jnp.float32) + b.astype(jnp.float32)
    return y.astype(x.dtype)


def _split_cols(p):
    idx, acc = [], 0
    for w in COLS[:-1]:
        acc += w
        idx.append(acc)
    return jnp.split(p, idx, axis=-1)


def _heads(t, h):
    return t.reshape(t.shape[:-1] + (h, t.shape[-1] // h))


def _neighborhood_attention(q, k, v, k_ctx, v_ctx, rpb):
    bsz, s, h, d = q.shape
    rows = s // GRID_W
    kh = min(WIN_H, rows)
    qg = q.reshape(bsz, rows, GRID_W, h, d) * (d ** -0.5)
    kg = k.reshape(bsz, rows, GRID_W, h, d)
    vg = v.reshape(bsz, rows, GRID_W, h, d)
    cols = jnp.arange(GRID_W)
    c_start = jnp.clip(cols - WIN_W // 2, 0, GRID_W - WIN_W)
    col_idx = c_start[:, None] + jnp.arange(WIN_W)[None, :]
    dc = col_idx - cols[:, None] + (WIN_W - 1)
    n_loc = kh * WIN_W

    def row_block(r):
        r_start = jnp.clip(r - kh // 2, 0, rows - kh)
        q_r = lax.dynamic_index_in_dim(qg, r, axis=1, keepdims=False)
        k_rows = lax.dynamic_slice_in_dim(kg, r_start, kh, axis=1)
        v_rows = lax.dynamic_slice_in_dim(vg, r_start, kh, axis=1)
        k_win = k_rows[:, :, col_idx]
        v_win = v_rows[:, :, col_idx]
        dr = r_start + jnp.arange(kh) - r + (WIN_H - 1)
        bias = rpb[:, dr[:, None, None], dc[None]]
        s_loc = jnp.einsum('bqhd,biqjhd->bhqij', q_r, k_win) + bias.transpose(0, 2, 1, 3)[None]
        s_ctx = jnp.einsum('bqhd,bkhd->bhqk', q_r, k_ctx)
        scores = jnp.concatenate([s_loc.reshape(bsz, h, GRID_W, n_loc), s_ctx], axis=-1)
        p = jax.nn.softmax(scores.astype(jnp.float32), axis=-1).astype(v.dtype)
        p_loc = p[..., :n_loc].reshape(bsz, h, GRID_W, kh, WIN_W)
        p_ctx = p[..., n_loc:]
        return (jnp.einsum('bhqij,biqjhd->bqhd', p_loc, v_win)
                + jnp.einsum('bhqk,bkhd->bqhd', p_ctx, v_ctx))

    o = lax.map(row_block, jnp.arange(rows))
    return o.transpose(1, 0, 2, 3, 4).reshape(bsz, s, h * d)


def _context_attention(q, k, v):
    d = q.shape[-1]
    scores = jnp.einsum('bqhd,bkhd->bhqk', q * (d ** -0.5), k)
    p = jax.nn.softmax(scores.astype(jnp.float32), axis=-1).astype(v.dtype)
    o = jnp.einsum('bhqk,bkhd->bqhd', p, v)
    return o.reshape(o.shape[:2] + (-1,))


def _axial_rope(x, row, col):
    inv_freq = ROPE_BASE ** (-jnp.arange(ROPE_PAIRS, dtype=jnp.float32) / ROPE_PAIRS)

    def rot(u, pos):
        ang = pos[:, None] * inv_freq[None, :]
        cos = jnp.cos(ang)[None, :, None, :]
        sin = jnp.sin(ang)[None, :, None, :]
        u1, u2 = u[..., :ROPE_PAIRS], u[..., ROPE_PAIRS:]
        return jnp.concatenate([u1 * cos - u2 * sin, u1 * sin + u2 * cos], axis=-1)

    half = x.shape[-1] // 2
    return jnp.concatenate([rot(x[..., :half], row), rot(x[..., half:], col)], axis=-1)


def _gla_log_decay(bg, w_gate2, b_gate):
    lr = bg.reshape(bg.shape[:-1] + (N_DIRS, GATE_RANK)).astype(jnp.float32)
    g = jnp.einsum('btdr,drk->btdk', lr, w_gate2.astype(jnp.float32)) + b_gate.astype(jnp.float32)
    g = jax.nn.log_sigmoid(g) / GATE_NORMALIZER
    g = g.reshape(g.shape[:3] + (GLA_HEADS, 2, ROPE_PAIRS))
    g = jnp.concatenate([g, g], axis=-1).reshape(g.shape[:3] + (GLA_HEADS, GLA_HEAD_K))
    return g[:, :, 0], g[:, :, 1]


def _gla_chunked(q, k, v, g, s0):
    bsz, t, h, _ = q.shape
    dv = v.shape[-1]
    n = t // GLA_CHUNK

    def chunks(a):
        return a.reshape(bsz, n, GLA_CHUNK, h, a.shape[-1]).transpose(1, 0, 3, 2, 4)

    tril = jnp.tril(jnp.ones((GLA_CHUNK, GLA_CHUNK), jnp.float32))

    def step(s, inp):
        qc, kc, vc, gc = inp
        cum = jnp.cumsum(gc, axis=2)
        last = cum[:, :, -1:]
        q_e = qc * jnp.exp(cum)
        k_e = kc * jnp.exp(-cum)
        att = jnp.einsum('bhik,bhjk->bhij', q_e, k_e) * tril
        o = jnp.einsum('bhij,bhjv->bhiv', att, vc) + jnp.einsum('bhik,bhkv->bhiv', q_e, s)
        s = (jnp.exp(last[:, :, 0])[..., None] * s
             + jnp.einsum('bhjk,bhjv->bhkv', kc * jnp.exp(last - cum), vc))
        return s, o

    s, o = lax.scan(step, s0, (chunks(q), chunks(k), chunks(v), chunks(g)))
    return o.transpose(1, 0, 3, 2, 4).reshape(bsz, t, h, dv), s


def _head_rms(o, g):
    y = o * lax.rsqrt(jnp.mean(jnp.square(o), axis=-1, keepdims=True) + RMS_EPS) * g.astype(jnp.float32)
    return y.reshape(y.shape[:2] + (-1,))


def _gla_mixer(bq, bk, bv, bg, bq_c, bk_c, bv_c, bg_c, w_gate2, b_gate, norm_g, need_ctx_out):
    f32 = jnp.float32
    t = bq.shape[1]
    pos = jnp.arange(t)
    row = (pos // GRID_W).astype(f32)
    col = (pos % GRID_W).astype(f32)
    scale = GLA_HEAD_K ** -0.5
    q = _axial_rope(_heads(bq.astype(f32), GLA_HEADS) * scale, row, col)
    k = _axial_rope(_heads(bk.astype(f32), GLA_HEADS), row, col)
    v = _heads(bv.astype(f32), GLA_HEADS)
    g_f, g_b = _gla_log_decay(bg, w_gate2, b_gate)
    qc = _heads(bq_c.astype(f32), GLA_HEADS) * scale
    kc = _heads(bk_c.astype(f32), GLA_HEADS)
    vc = _heads(bv_c.astype(f32), GLA_HEADS)
    gc_f, gc_b = _gla_log_decay(bg_c, w_gate2, b_gate)
    s0 = jnp.zeros((q.shape[0], GLA_HEADS, GLA_HEAD_K, GLA_HEAD_V), f32)

    def flip(a):
        return a[:, ::-1]

    oc_f, s_f = _gla_chunked(qc, kc, vc, gc_f, s0)
    oc_b, s_b = _gla_chunked(flip(qc), flip(kc), flip(vc), flip(gc_b), s0)
    o_f, _ = _gla_chunked(q, k, v, g_f, s_f)
    o_b, _ = _gla_chunked(flip(q), flip(k), flip(v), flip(g_b), s_b)
    y = _head_rms(o_f + flip(o_b), norm_g).astype(bv.dtype)
    y_c = _head_rms(oc_f + flip(oc_b), norm_g).astype(bv.dtype) if need_ctx_out else None
    return y, y_c


def _merge_branches(y_a, az, y_b, bz, mga, mgb, w_br_a, w_br_b, w_out):
    p_a = (y_a * jax.nn.silu(az)) @ w_br_a
    p_b = (y_b * jax.nn.silu(bz)) @ w_br_b
    return (jax.nn.sigmoid(mga) * p_a + jax.nn.sigmoid(mgb) * p_b) @ w_out


def setup_inputs(seed: int = 0) -> dict:
    key = jax.random.key(seed)
    ks = jax.random.split(key, 16)
    nrm = jax.random.normal
    f32 = jnp.float32
    return {
        "x": nrm(ks[0], (BATCH, SEQ, D_MODEL), f32),
        "c": nrm(ks[1], (BATCH, D_MODEL), f32),
        "ctx": nrm(ks[2], (BATCH, CTX_LEN, D_MODEL), f32),
        "c_ctx": nrm(ks[3], (D_MODEL,), f32),
        "w_mod": nrm(ks[4], (DEPTH, D_MODEL, 3 * D_MODEL), f32) * D_MODEL ** -0.5,
        "b_mod": 0.02 * nrm(ks[5], (DEPTH, 3 * D_MODEL), f32),
        "w_in": nrm(ks[6], (DEPTH, D_MODEL, PROJ_WIDTH), f32) * D_MODEL ** -0.5,
        "na_rpb": 0.1 * nrm(ks[7], (DEPTH, NA_HEADS, 2 * WIN_H - 1, 2 * WIN_W - 1), f32),
        "gla_w_gate2": nrm(ks[8], (DEPTH, N_DIRS, GATE_RANK, GATE_WIDTH), f32) * GATE_RANK ** -0.5,
        "gla_b_gate": 0.1 * nrm(ks[9], (DEPTH, N_DIRS, GATE_WIDTH), f32),
        "gla_norm_g": 1.0 + 0.02 * nrm(ks[10], (DEPTH, GLA_HEAD_V), f32),
        "w_br_a": nrm(ks[11], (DEPTH, NA_WIDTH, D_MODEL), f32) * NA_WIDTH ** -0.5 * DEEPNORM_BETA,
        "w_br_b": nrm(ks[12], (DEPTH, GLA_VAL_DIM, D_MODEL), f32) * GLA_VAL_DIM ** -0.5 * DEEPNORM_BETA,
        "w_out": nrm(ks[13], (DEPTH, D_MODEL, D_MODEL), f32) * D_MODEL ** -0.5 * DEEPNORM_BETA,
        "ln_g": 1.0 + 0.02 * nrm(ks[14], (DEPTH, D_MODEL), f32),
        "ln_b": 0.02 * nrm(ks[15], (DEPTH, D_MODEL), f32),
    }


def reference(x, c, ctx, c_ctx, w_mod, b_mod, w_in, na_rpb, gla_w_gate2, gla_b_gate,
              gla_norm_g, w_br_a, w_br_b, w_out, ln_g, ln_b):
    for l in range(DEPTH):
        update_ctx = l < DEPTH - 1
        mod = jax.nn.silu(c) @ w_mod[l] + b_mod[l]
        shift, scale, gate = jnp.split(mod[:, None, :], 3, axis=-1)
        mod_c = jax.nn.silu(c_ctx) @ w_mod[l] + b_mod[l]
        shift_c, scale_c, gate_c = jnp.split(mod_c, 3, axis=-1)
        h = _layer_norm(x) * (1.0 + scale) + shift
        h_c = _layer_norm(ctx) * (1.0 + scale_c) + shift_c
        aq, ak, av, az, bq, bk, bv, bz, bg, mga, mgb = _split_cols(h @ w_in[l])
        aq_c, ak_c, av_c, az_c, bq_c, bk_c, bv_c, bz_c, bg_c, mga_c, mgb_c = _split_cols(h_c @ w_in[l])
        k_ctx = _heads(ak_c, NA_HEADS)
        v_ctx = _heads(av_c, NA_HEADS)
        y_a = _neighborhood_attention(_heads(aq, NA_HEADS), _heads(ak, NA_HEADS), _heads(av, NA_HEADS),
                                      k_ctx, v_ctx, na_rpb[l])
        y_b, y_b_c = _gla_mixer(bq, bk, bv, bg, bq_c, bk_c, bv_c, bg_c,
                                gla_w_gate2[l], gla_b_gate[l], gla_norm_g[l], update_ctx)
        out = _merge_branches(y_a, az, y_b, bz, mga, mgb, w_br_a[l], w_br_b[l], w_out[l])
        x_new = _layer_norm(DEEPNORM_ALPHA * x + gate * out, ln_g[l], ln_b[l])
        if update_ctx:
            y_a_c = _context_attention(_heads(aq_c, NA_HEADS), k_ctx, v_ctx)
            out_c = _merge_branches(y_a_c, az_c, y_b_c, bz_c, mga_c, mgb_c, w_br_a[l], w_br_b[l], w_out[l])
            ctx = _layer_norm(DEEPNORM_ALPHA * ctx + gate_c * out_c, ln_g[l], ln_b[l])
        x = x_new
    return x
```

```python
import contextlib
import numpy as np
import concourse.bass as bass
import concourse.mybir as mybir
from concourse.bass_utils import run_bass_kernel_spmd

F32 = mybir.dt.float32
BF16 = mybir.dt.bfloat16
ALU = mybir.AluOpType
AF = mybir.ActivationFunctionType

ENGS = ("pe", "act", "dve", "pool", "sp")
NEG = -30000.0
LNQ = float(np.log(1.0 / 16.0))


class Prog:
    uid = 0

    def __init__(self, nc, bar, bar_base, G):
        self.nc = nc
        self.G = G
        self.ops = {e: [] for e in ENGS}
        self.last_w = {}
        self.readers = {}
        self.dma_cnt = {}
        self.bar = bar
        self.bar_base = bar_base

    def _deps(self, reads, writes):
        deps = []
        for t in reads:
            w = self.last_w.get(t)
            if w is not None:
                deps.append(w)
        for t in writes:
            w = self.last_w.get(t)
            if w is not None:
                deps.append(w)
            deps.extend(self.readers.get(t, ()))
        return deps

    def _record(self, ev, reads, writes):
        for t in reads:
            self.readers.setdefault(t, []).append(ev)
        for t in writes:
            self.last_w[t] = ev
            self.readers[t] = []

    def op(self, eng, fn, reads=(), writes=()):
        deps = self._deps(reads, writes)
        ev = ("E", eng, len(self.ops[eng]))
        self.ops[eng].append(dict(fn=fn, deps=deps, dma=None, sig=False))
        self._record(ev, reads, writes)
        return ev

    def dma(self, eng, fn, semkey, reads=(), writes=()):
        deps = self._deps(reads, writes)
        n = self.dma_cnt.get(semkey, 0) + 1
        self.dma_cnt[semkey] = n
        ev = ("D", semkey, 16 * n)
        self.ops[eng].append(dict(fn=fn, deps=deps, dma=semkey, sig=False))
        self._record(ev, reads, writes)
        return ev

    def emit(self):
        nc = self.nc
        G = self.G
        for e in ENGS:
            for o in self.ops[e]:
                for d in o["deps"]:
                    if d[0] == "E":
                        self.ops[d[1]][d[2]]["sig"] = True
            for o in reversed(self.ops[e]):
                if o["dma"] is None:
                    o["sig"] = True
                    break
        sigval = {}
        for e in ENGS:
            c = G["ebase"][e]
            arr = []
            for o in self.ops[e]:
                if o["sig"]:
                    c += 1
                arr.append(c)
            sigval[e] = arr
        esem = G["esem"]
        dmap = {}
        for i, k in enumerate(self.dma_cnt):
            assert i < len(G["dsem"]), "dma semaphore pool too small"
            dmap[k] = i
        dsem = {k: G["dsem"][i] for k, i in dmap.items()}
        dbase = {k: G["dbase"][i] for k, i in dmap.items()}
        with contextlib.ExitStack() as es:
            block = es.enter_context(nc.Block())

            def resolve(d):
                if d[0] == "E":
                    return (esem[d[1]], sigval[d[1]][d[2]], ("E", d[1]))
                return (dsem[d[1]], dbase[d[1]] + d[2], ("D", d[1]))

            def run(e, engobj):
                waited = {}
                for o in self.ops[e]:
                    need = {}
                    for d in o["deps"]:
                        sem, val, key = resolve(d)
                        if waited.get(key, 0) >= val:
                            continue
                        if need.get(key, (None, 0))[1] < val:
                            need[key] = (sem, val)
                    for key, (sem, val) in need.items():
                        engobj.wait_ge(sem, val)
                        waited[key] = val
                    inst = o["fn"](engobj)
                    if o["dma"] is not None:
                        inst.then_inc(dsem[o["dma"]], 16)
                    elif o["sig"]:
                        inst.then_inc(esem[e], 1)
                if sigval[e] and sigval[e][-1] > G["ebase"][e]:
                    engobj.wait_ge(esem[e], sigval[e][-1])
                if e == "sp":
                    for k, n in self.dma_cnt.items():
                        engobj.wait_ge(dsem[k], dbase[k] + 16 * n)
                engobj.sem_inc(self.bar, 1)
                engobj.wait_ge(self.bar, self.bar_base + len(ENGS))

            @block.tensor
            def _(eng):
                run("pe", eng)

            @block.scalar
            def _(eng):
                run("act", eng)

            @block.vector
            def _(eng):
                run("dve", eng)

            @block.gpsimd
            def _(eng):
                run("pool", eng)

            @block.sync
            def _(eng):
                run("sp", eng)
        for e in ENGS:
            if sigval[e]:
                G["ebase"][e] = sigval[e][-1]
        for k, i in dmap.items():
            G["dbase"][i] += 16 * self.dma_cnt[k]


C_AQ, C_AK, C_AV, C_AZ, C_BQ, C_BK, C_BV, C_BZ, C_BG, C_MGA, C_MGB = (
    0, 1024, 2048, 3072, 4096, 5120, 6144, 8192, 10240, 10272, 12320)


def build(stop_after=99, debug=()):
    nc = bass.Bass("TRN2", target_bir_lowering=False)
    din = lambda name, shape, dt=F32: nc.dram_tensor(name, list(shape), dt, kind="ExternalInput").ap()
    scr = lambda name, shape, dt=BF16: nc.dram_tensor(
        name, list(shape), dt, kind=("ExternalOutput" if name in debug else "Internal")).ap()

    xm = din("xm", [4096, 2048])
    xh = din("xh", [512, 2048])
    xc = din("xc", [256, 2048])
    cT = din("cT", [128, 32])
    wmod = din("wmod", [2048, 6144])
    bmodT = din("bmodT", [128, 32])
    bgate = din("bgate", [128, 2048])
    win = din("win", [2048, 14368])
    ropem = din("ropem", [8, 128, 4, 512])
    ropec = din("ropec", [1, 128, 4, 256])
    xp = din("xp", [2, 4096, 2048])
    ropep = din("ropep", [2, 8, 128, 4, 512])
    w2d_d = din("w2d", [17, 2048])
    bt2_d = din("bt2", [64, 16 * 8 * 128])
    msk_d = din("msk", [1, 8 * 6 * 128])
    cst_d = din("cst", [128, 2048])
    ngrep = din("ngrep", [64, 512])
    wbra = din("wbra", [1024, 2048])
    wbrb = din("wbrb", [2048, 2048])
    wout = din("wout", [2048, 2048])
    lngb = din("lngb", [128, 4096])
    yout = nc.dram_tensor("yout", [4096, 2048], F32, kind="ExternalOutput").ap()

    s_naq = scr("s_naq", [8, 128, 4096])
    s_nak = scr("s_nak", [8, 128, 4608])
    s_nav = scr("s_nav", [4608, 1040])
    s_az = scr("s_az", [4096, 1024])
    s_bz = scr("s_bz", [4096, 2048])
    s_bv = scr("s_bv", [4096, 2048])
    s_gq = scr("s_gq", [8, 128, 4096])
    s_gk = scr("s_gk", [8, 128, 4096])
    s_lr = scr("s_lr", [2, 16, 4096], F32)
    s_mg = scr("s_mg", [32, 128, 4096])
    s_of = scr("s_of", [4096, 2048], F32)
    s_zb = scr("s_zb", [4096, 2048])
    s_za = scr("s_za", [4096, 1024])
    c_nak = scr("c_nak", [8, 128, 256])
    c_nav = scr("c_nav", [256, 1040])
    c_gk = scr("c_gk", [8, 128, 256])
    c_bv = scr("c_bv", [256, 2048])
    c_lr = scr("c_lr", [2, 16, 256], F32)
    s_sctx = scr("s_sctx", [2, 128, 8 * 512], F32)
    s_pre = scr("s_pre", [2, 128, 8 * 512], F32)
    p_gk = scr("p_gk", [8, 128, 4096])
    p_bv = scr("p_bv", [4096, 2048])
    p_lr = scr("p_lr", [2, 16, 4096], F32)

    phase = [0]
    with contextlib.ExitStack() as top:
        uid = [0]

        def sb(es, name, shape, dt):
            uid[0] += 1
            return es.enter_context(nc.sbuf_tensor("%s_u%d" % (name, uid[0]), list(shape), dt))

        def ps(es, name, shape, dt):
            uid[0] += 1
            return es.enter_context(nc.psum_tensor("%s_u%d" % (name, uid[0]), list(shape), dt))
        bar = top.enter_context(nc.semaphore("phasebar"))
        G = dict(esem={e: top.enter_context(nc.semaphore("esem_" + e)) for e in ENGS},
                 ebase={e: 0 for e in ENGS},
                 dsem=[top.enter_context(nc.semaphore("dsem_%d" % i)) for i in range(48)],
                 dbase=[0] * 48)

        def newprog():
            p = Prog(nc, bar, phase[0] * len(ENGS), G)
            phase[0] += 1
            return p

        cst = sb(top, "cst", [128, 2048], F32)
        identb = sb(top, "identb", [128, 128], BF16)
        permb = sb(top, "permb", [128, 128], BF16)
        sc1 = sb(top, "sc1", [128, 32], F32)
        sh = sb(top, "sh", [128, 32], F32)
        gateb = sb(top, "gateb", [128, 2048], F32)
        w2d = sb(top, "w2d", [17, 2048], F32)
        kTc = sb(top, "kTc", [128, 8, 256], BF16)
        vc = sb(top, "vc", [128, 2, 1040], BF16)
        ident_f = cst[:, 0:128]
        perm_f = cst[:, 128:256]
        maskF = cst[0:64, 256:512]
        maskB = cst[0:64, 512:768]
        rstT = cst[:, 768:1280]
        usef = cst[:, 1280:1281]
        useb = cst[:, 1281:1282]
        lnq = cst[:, 1282:1283]
        zero1 = cst[:, 1283:1284]
        one1 = cst[:, 1284:1285]
        eps6 = cst[:, 1285:1286]
        fA = (cst[:, 1286:1287], cst[:, 1289:1290])
        f1 = (cst[:, 1287:1288], cst[:, 1290:1291])
        f2 = (cst[:, 1288:1289], cst[:, 1291:1292])

        with contextlib.ExitStack() as es:
            P = newprog()
            cTt = sb(es, "cTt", [128, 32], F32)
            scs = sb(es, "scs", [128, 32], F32)
            screp = sb(es, "screp", [128, 16, 128], F32)
            bmt = sb(es, "bmt", [128, 32], F32)
            wm = [sb(es, "wm%d" % i, [128, 16, 512], F32) for i in range(2)]
            pm = ps(es, "pm", [128, 64], F32)
            pg = ps(es, "pg", [128, 512], F32)
            P.dma("sp", lambda e: e.dma_start(out=cst[:], in_=cst_d), "cst", writes=["cst"])
            P.dma("sp", lambda e: e.dma_start(out=cTt[:], in_=cT), "cTt", writes=["cTt"])
            P.dma("sp", lambda e: e.dma_start(out=bmt[:], in_=bmodT), "bmt", writes=["bmt"])
            P.dma("sp", lambda e: e.dma_start(out=w2d[:], in_=w2d_d), "w2d", writes=["w2d"])
            P.dma("sp", lambda e: e.dma_start(out=gateb[:], in_=bgate), "gateb", writes=["gateb"])
            P.op("dve", lambda e: e.tensor_copy(out=identb[:], in_=ident_f), reads=["cst"], writes=["identb"])
            P.op("dve", lambda e: e.tensor_copy(out=permb[:], in_=perm_f), reads=["cst"], writes=["permb"])
            P.op("act", lambda e: e.activation(out=scs[:], in_=cTt[:], func=AF.Silu), reads=["cTt"], writes=["scs"])
            scs3 = scs[:].rearrange("p (k j) -> p k j", j=2)
            P.op("dve", lambda e: e.tensor_copy(out=screp[:], in_=scs3[:, :, 0:1].broadcast_to([128, 16, 128])),
                 reads=["scs"], writes=["screp"])
            for g in range(12):
                buf = wm[g % 2]
                bt = ("wm", g % 2)
                P.dma("sp", lambda e, g=g, buf=buf: e.dma_start(
                    out=buf[:], in_=wmod[:, g * 512:(g + 1) * 512].rearrange("(k p) n -> p k n", p=128)),
                    bt, writes=[bt])
                if g < 8:
                    for n in range(4):
                        ch = g * 4 + n
                        for k in range(16):
                            P.op("pe", lambda e, buf=buf, n=n, k=k, ch=ch: e.matmul(
                                pm[:, ch * 2:ch * 2 + 2], lhsT=buf[:, k, n * 128:(n + 1) * 128],
                                rhs=scs[:, k * 2:k * 2 + 2], start=(k == 0), stop=(k == 15)),
                                reads=[bt, "scs"], writes=["pm"])
                else:
                    for k in range(16):
                        P.op("pe", lambda e, buf=buf, k=k: e.matmul(
                            pg[:], lhsT=screp[:, k, :], rhs=buf[:, k, :], start=(k == 0), stop=(k == 15)),
                            reads=[bt, "screp"], writes=["pg"])
                    c0 = (g - 8) * 512
                    P.op("dve", lambda e, c0=c0: e.tensor_tensor(
                        out=gateb[:, c0:c0 + 512], in0=pg[:], in1=gateb[:, c0:c0 + 512], op=ALU.add),
                        reads=["pg", "gateb"], writes=["gateb"])
            pm3 = pm[:].rearrange("p (c j) -> p c j", j=2)
            sh3 = sh[:].rearrange("p (k j) -> p k j", j=2)
            sc3 = sc1[:].rearrange("p (k j) -> p k j", j=2)
            for j in range(2):
                P.op("dve", lambda e, j=j: e.tensor_tensor(out=sh3[:, :, j], in0=pm3[:, 0:16, j], in1=bmt[:, 0:16], op=ALU.add),
                     reads=["pm", "bmt"], writes=["sh"])
                P.op("dve", lambda e, j=j: e.scalar_tensor_tensor(out=sc3[:, :, j], in0=pm3[:, 16:32, j], scalar=1.0,
                                                                  in1=bmt[:, 16:32], op0=ALU.add, op1=ALU.add),
                     reads=["pm", "bmt"], writes=["sc1"])
            if "dbgA" in debug:
                dbgA = nc.dram_tensor("dbgA", [128, 2112], F32, kind="ExternalOutput").ap()
                P.dma("sp", lambda e: e.dma_start(out=dbgA[:, 0:2048], in_=gateb[:]), "dbg0", reads=["gateb"])
                P.dma("sp", lambda e: e.dma_start(out=dbgA[:, 2048:2080], in_=sc1[:]), "dbg1", reads=["sc1"])
                P.dma("sp", lambda e: e.dma_start(out=dbgA[:, 2080:2112], in_=sh[:]), "dbg2", reads=["sh"])
            P.emit()
        if stop_after <= 0:
            return nc

        def proj_phase(es, P, xsrc, ntok, modj, fm_specs, tm_specs, lr_dst):
            xt = [sb(es, "xt%d" % i, [128, 2048], F32) for i in range(2)]
            xn = [sb(es, "xn%d" % i, [128, 2048], BF16) for i in range(2)]
            st = sb(es, "st", [128, 4, 6], F32)
            mv = sb(es, "mv", [128, 2], F32)
            rs = sb(es, "rs", [128, 2], F32)
            hT = sb(es, "hT", [128, 16, 512], BF16)
            wt = [sb(es, "wt%d" % i, [128, 16, 512], BF16) for i in range(2)]
            wg = sb(es, "wg", [128, 16, 32], BF16)
            stg = [sb(es, "stg%d" % i, [128, 512], BF16) for i in range(3)]
            stv = [sb(es, "stv%d" % i, [128, 8, 65], BF16) for i in range(2)]
            stl = sb(es, "stl", [16, 2, 512], F32)
            tp = [ps(es, "tp%d" % i, [128, 4, 128], F32) for i in range(2)]
            acc = [ps(es, "acc%d" % i, [128, 512], F32) for i in range(3)]
            accl = ps(es, "accl", [16, 2, 512], F32)
            for i in range(2):
                P.op("dve", lambda e, i=i: e.memset(stv[i][:], 1.0), writes=[("stv", i)])
            cnt = dict(w=0, stg=0, stv=0, acc=0, x=0, ev=0)
            ntile = (ntok + 511) // 512
            for t in range(ntile):
                tok0 = t * 512
                T = min(512, ntok - tok0)
                nsub = T // 128
                import os as _os2
                _k2 = _os2.environ.get("KDBG2", "")
                if _k2 == "skipln":
                    P.op("dve", lambda e: e.memset(hT[:], 0.5), writes=[("hT", s) for s in range(nsub)])
                for s in (range(nsub) if _k2 != "skipln" else []):
                    xi = cnt["x"] % 2
                    cnt["x"] += 1
                    r0 = tok0 + s * 128
                    P.dma("sp", lambda e, xi=xi, r0=r0: e.dma_start(out=xt[xi][:], in_=xsrc[r0:r0 + 128, :]),
                          ("xt", xi), writes=[("xt", xi)])
                    for j in range(4):
                        P.op("dve", lambda e, xi=xi, j=j: e.bn_stats(out=st[:, j, :], in_=xt[xi][:, j * 512:(j + 1) * 512]),
                             reads=[("xt", xi)], writes=["st"])
                    P.op("dve", lambda e: e.bn_aggr(out=mv[:], in_=st[:].rearrange("p a b -> p (a b)")),
                         reads=["st"], writes=["mv"])
                    P.op("dve", lambda e: e.tensor_scalar(out=rs[:, 0:1], in0=mv[:, 1:2], scalar1=1e-6, scalar2=None, op0=ALU.add),
                         reads=["mv"], writes=["rs0"])
                    P.op("act", lambda e: e.activation(out=rs[:, 0:1], in_=rs[:, 0:1], func=AF.Sqrt), reads=["rs0"], writes=["rs0"])
                    P.op("dve", lambda e: e.reciprocal(out=rs[:, 1:2], in_=rs[:, 0:1]), reads=["rs0"], writes=["rs1"])
                    P.op("dve", lambda e, xi=xi: e.tensor_scalar(out=xn[xi][:], in0=xt[xi][:], scalar1=mv[:, 0:1], scalar2=rs[:, 1:2],
                                                                op0=ALU.subtract, op1=ALU.mult),
                         reads=[("xt", xi), "mv", "rs1"], writes=[("xn", xi)])
                    for kq in range(4):
                        tb = kq % 2
                        for kk in range(4):
                            k = kq * 4 + kk
                            P.op("pe", lambda e, xi=xi, k=k, kk=kk, tb=tb: e.matmul(
                                tp[tb][:, kk, :], lhsT=xn[xi][:, k * 128:(k + 1) * 128], rhs=identb[:], start=True, stop=True),
                                reads=[("xn", xi), "identb"], writes=[("tp", tb)])
                        for kk in range(4):
                            k = kq * 4 + kk
                            P.op("act", lambda e, k=k, kk=kk, tb=tb, s=s: e.activation(
                                out=hT[:, k, s * 128:(s + 1) * 128], in_=tp[tb][:, kk, :], func=AF.Identity,
                                scale=sc1[:, 2 * k + modj:2 * k + modj + 1], bias=sh[:, 2 * k + modj:2 * k + modj + 1]),
                                reads=[("tp", tb), "sc1", "sh"], writes=[("hT", s)])
                hTr = [("hT", s) for s in range(nsub)]
                import os as _os
                _kd = _os.environ.get("KDBG", "fm,tm,lr").split(",")
                if "fm" not in _kd:
                    fm_specs = []
                if "tm" not in _kd:
                    tm_specs = []
                if "lr" not in _kd:
                    lr_dst = None
                for (col0, ng, dst, dch0, dtok0, scale) in fm_specs:
                    if dst is None:
                        continue
                    for g in range(ng):
                        wi = cnt["w"] % 2
                        cnt["w"] += 1
                        c0 = col0 + g * 512
                        P.dma("pool", lambda e, wi=wi, c0=c0: e.dma_start(
                            out=wt[wi][:], in_=win[:, c0:c0 + 512].rearrange("(k p) n -> p k n", p=128)),
                            ("wt", wi), writes=[("wt", wi)])
                        for n in range(4):
                            ai = cnt["acc"] % 3
                            cnt["acc"] += 1
                            for k in range(16):
                                P.op("pe", lambda e, wi=wi, n=n, k=k, ai=ai, T=T: e.matmul(
                                    acc[ai][:, 0:T], lhsT=wt[wi][:, k, n * 128:(n + 1) * 128], rhs=hT[:, k, 0:T],
                                    start=(k == 0), stop=(k == 15)),
                                    reads=[("wt", wi)] + hTr, writes=[("acc", ai)])
                            si = cnt["stg"] % 3
                            cnt["stg"] += 1
                            cnt["ev"] += 1
                            if cnt["ev"] % 2 == 0:
                                P.op("act", lambda e, si=si, ai=ai, T=T, scale=scale: e.activation(
                                    out=stg[si][:, 0:T], in_=acc[ai][:, 0:T], func=AF.Copy, scale=float(scale)),
                                    reads=[("acc", ai)], writes=[("stg", si)])
                            else:
                                P.op("dve", lambda e, si=si, ai=ai, T=T, scale=scale: e.tensor_scalar(
                                    out=stg[si][:, 0:T], in0=acc[ai][:, 0:T], scalar1=float(scale), scalar2=None, op0=ALU.mult),
                                    reads=[("acc", ai)], writes=[("stg", si)])
                            ch = dch0 + g * 4 + n
                            P.dma("sp", lambda e, si=si, ch=ch, T=T, dst=dst, a=dtok0 + tok0: e.dma_start(
                                out=dst[ch, :, a:a + T], in_=stg[si][:, 0:T]),
                                ("stg", si), reads=[("stg", si)])
                if lr_dst is not None:
                    P.dma("pool", lambda e: e.dma_start(
                        out=wg[:], in_=win[:, C_BG:C_BG + 32].rearrange("(k p) n -> p k n", p=128)),
                        "wg", writes=["wg"])
                    for d in range(2):
                        for k in range(16):
                            P.op("pe", lambda e, d=d, k=k, T=T: e.matmul(
                                accl[:, d, 0:T], lhsT=wg[:, k, d * 16:(d + 1) * 16], rhs=hT[:, k, 0:T],
                                start=(k == 0), stop=(k == 15)),
                                reads=["wg"] + hTr, writes=["accl"])
                    P.op("dve", lambda e, T=T: e.tensor_copy(out=stl[:, :, 0:T], in_=accl[:, :, 0:T]), reads=["accl"], writes=["stl"])
                    P.dma("sp", lambda e, T=T, a=tok0: e.dma_start(
                        out=lr_dst[:, :, a:a + T].rearrange("d r t -> r d t"), in_=stl[:, :, 0:T]),
                        "stl", reads=["stl"])
                for (col0, ng, dst, dcol0, dtok0, kind) in tm_specs:
                    if dst is None:
                        continue
                    for g in range(ng):
                        wi = cnt["w"] % 2
                        cnt["w"] += 1
                        c0 = col0 + g * 512
                        P.dma("pool", lambda e, wi=wi, c0=c0: e.dma_start(
                            out=wt[wi][:], in_=win[:, c0:c0 + 512].rearrange("(k p) n -> p k n", p=128)),
                            ("wt", wi), writes=[("wt", wi)])
                        for s in range(nsub):
                            ai = cnt["acc"] % 3
                            cnt["acc"] += 1
                            for k in range(16):
                                P.op("pe", lambda e, wi=wi, s=s, k=k, ai=ai: e.matmul(
                                    acc[ai][:], lhsT=hT[:, k, s * 128:(s + 1) * 128], rhs=wt[wi][:, k, :],
                                    start=(k == 0), stop=(k == 15)),
                                    reads=[("wt", wi), ("hT", s)], writes=[("acc", ai)])
                            a = dtok0 + tok0 + s * 128
                            cnt["ev"] += 1
                            eng = "act" if cnt["ev"] % 2 == 0 else "dve"
                            if kind == "v65":
                                vi = cnt["stv"] % 2
                                cnt["stv"] += 1
                                src = acc[ai][:].rearrange("p (h d) -> p h d", d=64)
                                if eng == "act":
                                    P.op("act", lambda e, vi=vi, src=src: e.activation(out=stv[vi][:, :, 0:64], in_=src, func=AF.Copy),
                                         reads=[("acc", ai)], writes=[("stv", vi)])
                                else:
                                    P.op("dve", lambda e, vi=vi, src=src: e.tensor_copy(out=stv[vi][:, :, 0:64], in_=src),
                                         reads=[("acc", ai)], writes=[("stv", vi)])
                                h0 = (dcol0 + g * 512) // 64
                                P.dma("sp", lambda e, vi=vi, a=a, h0=h0, dst=dst: e.dma_start(
                                    out=dst[a:a + 128, h0 * 65:(h0 + 8) * 65], in_=stv[vi][:].rearrange("p h d -> p (h d)")),
                                    ("stv", vi), reads=[("stv", vi)])
                            else:
                                si = cnt["stg"] % 3
                                cnt["stg"] += 1
                                if eng == "act":
                                    P.op("act", lambda e, si=si, ai=ai: e.activation(out=stg[si][:], in_=acc[ai][:], func=AF.Copy),
                                         reads=[("acc", ai)], writes=[("stg", si)])
                                else:
                                    P.op("dve", lambda e, si=si, ai=ai: e.tensor_copy(out=stg[si][:], in_=acc[ai][:]),
                                         reads=[("acc", ai)], writes=[("stg", si)])
                                cc = dcol0 + g * 512
                                P.dma("sp", lambda e, si=si, a=a, cc=cc, dst=dst: e.dma_start(
                                    out=dst[a:a + 128, cc:cc + 512], in_=stg[si][:]),
                                    ("stg", si), reads=[("stg", si)])

        with contextlib.ExitStack() as es:
            P = newprog()
            proj_phase(es, P, xc, 256, 1,
                       fm_specs=[(C_AK, 2, c_nak, 0, 0, 1.0), (C_BK, 2, c_gk, 0, 0, 1.0)],
                       tm_specs=[(C_AV, 2, c_nav, 0, 0, "v65"), (C_BV, 4, c_bv, 0, 0, "plain")],
                       lr_dst=c_lr)
            P.emit()
        if stop_after <= 1:
            return nc

        with contextlib.ExitStack() as es:
            P = newprog()
            proj_phase(es, P, xm, 4096, 0,
                       fm_specs=[(C_AQ, 2, s_naq, 0, 0, 0.125), (C_AK, 2, s_nak, 0, 256, 1.0),
                                 (C_BQ, 2, s_gq, 0, 0, 1.0), (C_BK, 2, s_gk, 0, 0, 1.0),
                                 (C_MGA, 4, s_mg, 0, 0, 1.0), (C_MGB, 4, s_mg, 16, 0, 1.0)],
                       tm_specs=[(C_AV, 2, s_nav, 0, 256, "v65"), (C_AZ, 2, s_az, 0, 0, "plain"),
                                 (C_BV, 4, s_bv, 0, 0, "plain"), (C_BZ, 4, s_bz, 0, 0, "plain")],
                       lr_dst=s_lr)
            P.emit()
        for part, (src_lo, ext_lo) in enumerate(((0, 0), (256, 4352))):
            with contextlib.ExitStack() as es:
                P = newprog()
                proj_phase(es, P, xh[src_lo:src_lo + 256, :], 256, 0,
                           fm_specs=[(C_AK, 2, s_nak, 0, ext_lo, 1.0)],
                           tm_specs=[(C_AV, 2, s_nav, 0, ext_lo, "v65")],
                           lr_dst=None)
                P.emit()
        if stop_after <= 2:
            return nc

        def gla_phase(es, P, src, T, ntiles, d, outputs, s_init, s_final, s_mix=None):
            nch = T // 64
            gkt = sb(es, "gkt", [128, 8, 512], BF16)
            gqt = sb(es, "gqt", [128, 8, 512], BF16) if outputs else None
            ropet = sb(es, "ropet", [128, 4, 512], F32)
            lrt = sb(es, "lrt", [17, 512], F32)
            spt = [sb(es, "spt%d" % i, [128, 512], F32) for i in range(2)]
            ct = [sb(es, "ct%d" % i, [128, 512], F32) for i in range(2)]
            cl = sb(es, "cl", [128, 8, 8], F32)
            nb = sb(es, "nb", [128, 8, 8], F32)
            pbs = sb(es, "pbs", [128, 8, 8], F32)
            nbq = sb(es, "nbq", [128, 8, 8], F32)
            el = sb(es, "el", [128, 8, 8], F32)
            Eq = sb(es, "Eq", [128, 512], F32)
            Ek = sb(es, "Ek", [128, 512], F32)
            El = sb(es, "El", [128, 512], F32)
            t1 = [sb(es, "t1_%d" % i, [128, 512], F32) for i in range(2)]
            t2 = [sb(es, "t2_%d" % i, [128, 512], F32) for i in range(2)]
            rk = [sb(es, "rk_%d" % i, [128, 512], F32) for i in range(2)]
            qe = sb(es, "qe", [128, 8, 512], BF16) if outputs else None
            ke = sb(es, "ke", [128, 8, 512], BF16) if outputs else None
            kl = sb(es, "kl", [128, 8, 512], BF16)
            klT = sb(es, "klT", [64, 8, 8, 128], BF16)
            vt = [sb(es, "vt%d" % i, [64, 2048], BF16) for i in range(2)]
            S = sb(es, "S", [128, 8, 512], F32)
            Sbf = sb(es, "Sbf", [128, 8, 512], BF16) if outputs else None
            pz = ps(es, "pz", [128, 512], F32)
            pr = ps(es, "pr", [128, 512], F32)
            ptk = ps(es, "ptk", [64, 4, 128], F32)
            if outputs:
                attS = sb(es, "attS", [64, 4, 64], BF16)
                ost = sb(es, "ost", [64, 2048], F32)
                pa = ps(es, "pa", [64, 4, 64], F32)
                po = [ps(es, "po%d" % i, [64, 512], F32) for i in range(4)]
                if d == 1:
                    oft = sb(es, "oft", [64, 2048], F32)
                    bzt = sb(es, "bzt", [64, 2048], BF16)
                    sz = sb(es, "sz", [64, 2048], F32)
                    junk = sb(es, "junk", [64, 512], F32)
                    ssq = sb(es, "ssq", [64, 4], F32)
                    rr = sb(es, "rr", [64, 4], F32)
                    ngt = sb(es, "ngt", [64, 512], F32)
                    zbt = sb(es, "zbt", [64, 2048], BF16)
                    P.dma("sp", lambda e: e.dma_start(out=ngt[:], in_=ngrep), "ngt", writes=["ngt"])
            mask = (maskF if d == 0 else maskB).rearrange("p (h i) -> p h i", i=64)
            if s_init is None:
                P.op("pool", lambda e: e.memset(S[:], 0.0), writes=[("S", g) for g in range(8)])
            else:
                sap, use = s_init
                P.dma("sp", lambda e: e.dma_start(out=S[:].rearrange("p g n -> p (g n)"), in_=sap), "Sld",
                      writes=[("S", g) for g in range(8)])
                P.op("dve", lambda e: e.tensor_scalar(out=S[:], in0=S[:], scalar1=use, scalar2=None, op0=ALU.mult),
                     reads=[("S", g) for g in range(8)] + ["cst"], writes=[("S", g) for g in range(8)])
            if outputs:
                P.op("pool", lambda e: e.tensor_copy(out=Sbf[:], in_=S[:]), reads=[("S", g) for g in range(8)],
                     writes=[("Sbf", g) for g in range(8)])
            P.op("pool", lambda e: e.memset(lrt[:], 1.0), writes=["lrt"])
            tiles = list(range(ntiles))
            chunks = list(range(nch))
            if d == 1:
                tiles.reverse()
                chunks.reverse()
            sgn = 1.0 if d == 0 else -1.0
            vcnt = 0
            for t in tiles:
                tok0 = t * T
                P.dma("sp", lambda e, tok0=tok0: e.dma_start(out=gkt[:, :, 0:T], in_=src["gk"][:, :, tok0:tok0 + T].rearrange("g p t -> p g t")),
                      "gkt", writes=["gkt"])
                if outputs:
                    P.dma("sp", lambda e, tok0=tok0: e.dma_start(out=gqt[:, :, 0:T], in_=src["gq"][:, :, tok0:tok0 + T].rearrange("g p t -> p g t")),
                          "gqt", writes=["gqt"])
                P.dma("sp", lambda e, t=t: e.dma_start(out=ropet[:, :, 0:T], in_=src["rope"][t]), "ropet", writes=["ropet"])
                P.dma("sp", lambda e, tok0=tok0: e.dma_start(out=lrt[0:16, 0:T], in_=src["lr"][d, :, tok0:tok0 + T]), "lrt", writes=["lrt"])
                for g in range(8):
                    a = g % 2
                    i2 = g % 2
                    wcol = (d * 8 + g) * 128
                    P.op("pe", lambda e, wcol=wcol: e.matmul(pz[:, 0:T], lhsT=w2d[:, wcol:wcol + 128], rhs=lrt[:, 0:T], start=True, stop=True),
                         reads=["w2d", "lrt"], writes=["pz"])
                    P.op("act", lambda e, i2=i2: e.activation(out=spt[i2][:, 0:T], in_=pz[:, 0:T], func=AF.Exp, scale=-1.0),
                         reads=["pz"], writes=[("spt", i2)])
                    P.op("act", lambda e, i2=i2: e.activation(out=spt[i2][:, 0:T], in_=spt[i2][:, 0:T], func=AF.Ln, bias=one1),
                         reads=[("spt", i2), "cst"], writes=[("spt", i2)])
                    P.op("dve", lambda e, i2=i2: e.tensor_tensor_scan(out=ct[i2][:, 0:T], data0=rstT[:, 0:T], data1=spt[i2][:, 0:T],
                                                                      initial=0.0, op0=ALU.mult, op1=ALU.add),
                         reads=[("spt", i2), "cst"], writes=[("ct", i2)])
                    c3 = ct[i2][:, 0:T].rearrange("p (c t) -> p c t", t=64)
                    P.op("dve", lambda e, g=g, c3=c3: e.tensor_copy(out=cl[:, g, 0:nch], in_=c3[:, :, 63]),
                         reads=[("ct", i2)], writes=[("cl", g)])
                    P.op("dve", lambda e, g=g: e.tensor_scalar(out=nb[:, g, 0:nch], in0=cl[:, g, 0:nch], scalar1=-1.0 / 16, scalar2=None, op0=ALU.mult),
                         reads=[("cl", g)], writes=[("nb", g)])
                    P.op("act", lambda e, g=g: e.activation(out=el[:, g, 0:nch], in_=nb[:, g, 0:nch], func=AF.Exp),
                         reads=[("nb", g)], writes=[("el", g)])
                    if d == 1:
                        P.op("dve", lambda e, g=g: e.tensor_scalar(out=pbs[:, g, 0:nch], in0=cl[:, g, 0:nch], scalar1=1.0 / 16, scalar2=None, op0=ALU.mult),
                             reads=[("cl", g)], writes=[("pbs", g)])
                        P.op("dve", lambda e, g=g: e.tensor_scalar(out=nbq[:, g, 0:nch], in0=cl[:, g, 0:nch], scalar1=-1.0 / 16, scalar2=LNQ,
                                                                   op0=ALU.mult, op1=ALU.add),
                             reads=[("cl", g)], writes=[("nbq", g)])
                        P.op("dve", lambda e, i2=i2: e.tensor_tensor(out=ct[i2][:, 0:T], in0=ct[i2][:, 0:T], in1=spt[i2][:, 0:T], op=ALU.subtract),
                             reads=[("ct", i2), ("spt", i2)], writes=[("ct", i2)])
                    if d == 0:
                        if outputs:
                            P.op("act", lambda e, i2=i2: e.activation(out=Eq[:, 0:T], in_=ct[i2][:, 0:T], func=AF.Exp, scale=-1.0 / 16, bias=lnq),
                                 reads=[("ct", i2), "cst"], writes=["Eq"])
                            P.op("act", lambda e, i2=i2: e.activation(out=Ek[:, 0:T], in_=ct[i2][:, 0:T], func=AF.Exp, scale=1.0 / 16),
                                 reads=[("ct", i2)], writes=["Ek"])
                        for ch in range(nch):
                            P.op("act", lambda e, i2=i2, g=g, ch=ch: e.activation(
                                out=El[:, ch * 64:(ch + 1) * 64], in_=ct[i2][:, ch * 64:(ch + 1) * 64], func=AF.Exp, scale=1.0 / 16,
                                bias=nb[:, g, ch:ch + 1]), reads=[("ct", i2), ("nb", g)], writes=["El"])
                    else:
                        P.op("act", lambda e, i2=i2: e.activation(out=El[:, 0:T], in_=ct[i2][:, 0:T], func=AF.Exp, scale=-1.0 / 16),
                             reads=[("ct", i2)], writes=["El"])
                        if outputs:
                            for ch in range(nch):
                                P.op("act", lambda e, i2=i2, g=g, ch=ch: e.activation(
                                    out=Eq[:, ch * 64:(ch + 1) * 64], in_=ct[i2][:, ch * 64:(ch + 1) * 64], func=AF.Exp, scale=1.0 / 16,
                                    bias=nbq[:, g, ch:ch + 1]), reads=[("ct", i2), ("nbq", g)], writes=["Eq"])
                                P.op("act", lambda e, i2=i2, g=g, ch=ch: e.activation(
                                    out=Ek[:, ch * 64:(ch + 1) * 64], in_=ct[i2][:, ch * 64:(ch + 1) * 64], func=AF.Exp, scale=-1.0 / 16,
                                    bias=pbs[:, g, ch:ch + 1]), reads=[("ct", i2), ("pbs", g)], writes=["Ek"])
                    for which in (("k", "q") if outputs else ("k",)):
                        u = gkt if which == "k" else gqt
                        ut = "gkt" if which == "k" else "gqt"
                        P.op("pe", lambda e, u=u, g=g: e.matmul(pr[:, 0:T], lhsT=permb[:], rhs=u[:, g, 0:T], start=True, stop=True),
                             reads=["permb", ut], writes=["pr"])
                        P.op("pool", lambda e, u=u, g=g, a=a, i2=i2: e.tensor_tensor(out=t1[i2][:, 0:T], in0=u[:, g, 0:T], in1=ropet[:, a, 0:T], op=ALU.mult),
                             reads=[ut, "ropet"], writes=[("t1", i2)])
                        P.op("dve", lambda e, a=a, i2=i2: e.tensor_tensor(out=t2[i2][:, 0:T], in0=pr[:, 0:T], in1=ropet[:, 2 + a, 0:T], op=ALU.mult),
                             reads=["pr", "ropet"], writes=[("t2", i2)])
                        P.op("pool", lambda e, i2=i2: e.tensor_tensor(out=rk[i2][:, 0:T], in0=t1[i2][:, 0:T], in1=t2[i2][:, 0:T], op=ALU.add),
                             reads=[("t1", i2), ("t2", i2)], writes=[("rk", i2)])
                        if which == "k":
                            P.op("pool", lambda e, g=g, i2=i2: e.tensor_tensor(out=kl[:, g, 0:T], in0=rk[i2][:, 0:T], in1=El[:, 0:T], op=ALU.mult),
                                 reads=[("rk", i2), "El"], writes=[("kl", g)])
                            if outputs:
                                P.op("dve", lambda e, g=g, i2=i2: e.tensor_tensor(out=ke[:, g, 0:T], in0=rk[i2][:, 0:T], in1=Ek[:, 0:T], op=ALU.mult),
                                     reads=[("rk", i2), "Ek"], writes=[("ke", g)])
                        else:
                            P.op("dve", lambda e, g=g, i2=i2: e.tensor_tensor(out=qe[:, g, 0:T], in0=rk[i2][:, 0:T], in1=Eq[:, 0:T], op=ALU.mult),
                                 reads=[("rk", i2), "Eq"], writes=[("qe", g)])
                for ch in range(nch):
                    for half in range(2):
                        for j in range(4):
                            g = half * 4 + j
                            P.op("pe", lambda e, g=g, j=j, ch=ch: e.matmul(ptk[:, j, :], lhsT=kl[:, g, ch * 64:(ch + 1) * 64], rhs=identb[:],
                                                                        start=True, stop=True),
                                 reads=[("kl", g), "identb"], writes=["ptk"])
                        P.op("act", lambda e, ch=ch, half=half: e.activation(out=klT[:, ch, half * 4:(half + 1) * 4, :], in_=ptk[:], func=AF.Copy),
                             reads=["ptk"], writes=[("klT", ch)])
                for ch in chunks:
                    r0 = tok0 + ch * 64
                    vi = vcnt % 2
                    vcnt += 1
                    P.dma("sp", lambda e, vi=vi, r0=r0: e.dma_start(out=vt[vi][:], in_=src["bv"][r0:r0 + 64, :]), ("vt", vi), writes=[("vt", vi)])
                    if outputs:
                        for hd in range(4):
                            for a in range(2):
                                g = hd * 2 + a
                                P.op("pe", lambda e, hd=hd, g=g, a=a, ch=ch: e.matmul(
                                    pa[:, hd, :], lhsT=ke[:, g, ch * 64:(ch + 1) * 64], rhs=qe[:, g, ch * 64:(ch + 1) * 64],
                                    start=(a == 0), stop=(a == 1)), reads=[("ke", g), ("qe", g)], writes=["pa"])
                        P.op("dve", lambda e: e.tensor_tensor(out=attS[:], in0=pa[:], in1=mask, op=ALU.mult),
                             reads=["pa", "cst"], writes=["attS"])
                        for hd in range(4):
                            P.op("pe", lambda e, hd=hd, vi=vi: e.matmul(po[hd][:], lhsT=attS[:, hd, :], rhs=vt[vi][:, hd * 512:(hd + 1) * 512],
                                                                       start=True, stop=False),
                                 reads=["attS", ("vt", vi)], writes=[("po", hd)])
                            for a in range(2):
                                g = hd * 2 + a
                                P.op("pe", lambda e, hd=hd, g=g, a=a, ch=ch: e.matmul(
                                    po[hd][:], lhsT=qe[:, g, ch * 64:(ch + 1) * 64], rhs=Sbf[:, g, :], start=False, stop=(a == 1)),
                                    reads=[("qe", g), ("Sbf", g)], writes=[("po", hd)])
                        if d == 0:
                            for hd in range(4):
                                P.op("act", lambda e, hd=hd: e.activation(out=ost[:, hd * 512:(hd + 1) * 512], in_=po[hd][:], func=AF.Copy),
                                     reads=[("po", hd)], writes=["ost"])
                            P.dma("sp", lambda e, r0=r0: e.dma_start(out=src["of"][r0:r0 + 64, :], in_=ost[:]), "ost", reads=["ost"])
                        else:
                            P.dma("sp", lambda e, r0=r0: e.dma_start(out=oft[:], in_=src["of"][r0:r0 + 64, :]), "oft", writes=["oft"])
                            P.dma("sp", lambda e, r0=r0: e.dma_start(out=bzt[:], in_=src["bz"][r0:r0 + 64, :]), "bzt", writes=["bzt"])
                            P.op("act", lambda e: e.activation(out=sz[:], in_=bzt[:], func=AF.Silu), reads=["bzt"], writes=["sz"])
                            for hd in range(4):
                                hs = slice(hd * 512, (hd + 1) * 512)
                                P.op("dve", lambda e, hd=hd, hs=hs: e.tensor_tensor(out=ost[:, hs], in0=po[hd][:], in1=oft[:, hs], op=ALU.add),
                                     reads=[("po", hd), "oft"], writes=[("osth", hd)])
                                P.op("act", lambda e, hd=hd, hs=hs: e.activation(out=junk[:], in_=ost[:, hs], func=AF.Square,
                                                                                 accum_out=ssq[:, hd:hd + 1]),
                                     reads=[("osth", hd)], writes=["junk", ("ssq", hd)])
                            P.op("dve", lambda e: e.tensor_scalar(out=rr[:], in0=ssq[:], scalar1=1.0 / 512, scalar2=1e-6, op0=ALU.mult, op1=ALU.add),
                                 reads=[("ssq", h) for h in range(4)], writes=["rr"])
                            P.op("act", lambda e: e.activation(out=rr[:], in_=rr[:], func=AF.Sqrt), reads=["rr"], writes=["rr"])
                            P.op("dve", lambda e: e.reciprocal(out=rr[:], in_=rr[:]), reads=["rr"], writes=["rr"])
                            for hd in range(4):
                                hs = slice(hd * 512, (hd + 1) * 512)
                                P.op("dve", lambda e, hd=hd, hs=hs: e.scalar_tensor_tensor(
                                    out=ost[:, hs], in0=ost[:, hs], scalar=rr[:, hd:hd + 1], in1=ngt[:], op0=ALU.mult, op1=ALU.mult),
                                    reads=[("osth", hd), "rr", "ngt"], writes=[("osth", hd)])
                            P.op("pool", lambda e: e.tensor_tensor(out=zbt[:], in0=ost[:], in1=sz[:], op=ALU.mult),
                                 reads=[("osth", h) for h in range(4)] + ["sz"], writes=["zbt"])
                            P.dma("sp", lambda e, r0=r0: e.dma_start(out=src["zb"][r0:r0 + 64, :], in_=zbt[:]), "zbt", reads=["zbt"])
                    for g in range(8):
                        hd = g // 2
                        pu = pz if g % 2 == 0 else pr
                        put = "pz" if g % 2 == 0 else "pr"
                        P.op("pe", lambda e, g=g, hd=hd, pu=pu, ch=ch, vi=vi: e.matmul(
                            pu[:], lhsT=klT[:, ch, g, :], rhs=vt[vi][:, hd * 512:(hd + 1) * 512], start=True, stop=True),
                            reads=[("klT", ch), ("vt", vi)], writes=[put])
                        P.op("dve", lambda e, g=g, pu=pu, ch=ch: e.scalar_tensor_tensor(
                            out=S[:, g, :], in0=S[:, g, :], scalar=el[:, g, ch:ch + 1], in1=pu[:], op0=ALU.mult, op1=ALU.add),
                            reads=[("S", g), ("el", g), put], writes=[("S", g)])
                        if outputs:
                            P.op("pool", lambda e, g=g: e.tensor_copy(out=Sbf[:, g, :], in_=S[:, g, :]),
                                 reads=[("S", g)], writes=[("Sbf", g)])
            if s_mix is not None:
                ap2, m1, m2 = s_mix
                Smx = sb(es, "Smx", [128, 8, 512], F32)
                allS = [("S", g) for g in range(8)]
                P.dma("sp", lambda e: e.dma_start(out=Smx[:].rearrange("p g n -> p (g n)"), in_=ap2), "Smx", writes=["Smx"])
                P.op("dve", lambda e: e.tensor_scalar(out=S[:], in0=S[:], scalar1=m1, scalar2=None, op0=ALU.mult),
                     reads=allS + ["cst"], writes=allS)
                P.op("dve", lambda e: e.scalar_tensor_tensor(out=S[:], in0=Smx[:], scalar=m2, in1=S[:], op0=ALU.mult, op1=ALU.add),
                     reads=allS + ["Smx", "cst"], writes=allS)
            if s_final is not None:
                P.dma("sp", lambda e: e.dma_start(out=s_final, in_=S[:].rearrange("p g n -> p (g n)")), "Sst",
                      reads=[("S", g) for g in range(8)])

        csrc = dict(gk=c_gk, lr=c_lr, bv=c_bv, rope=ropec)
        msrc = dict(gk=s_gk, gq=s_gq, lr=s_lr, bv=s_bv, bz=s_bz, rope=ropem, of=s_of, zb=s_zb)
        for d in range(2):
            with contextlib.ExitStack() as es:
                P = newprog()
                gla_phase(es, P, csrc, 256, 1, d, False, None, s_sctx[d])
                P.emit()
        if stop_after <= 3:
            return nc
        for d in range(2):
            with contextlib.ExitStack() as es:
                P = newprog()
                proj_phase(es, P, xp[d], 4096, 0,
                           fm_specs=[(C_BK, 2, p_gk, 0, 0, 1.0)],
                           tm_specs=[(C_BV, 4, p_bv, 0, 0, "plain")],
                           lr_dst=p_lr)
                P.emit()
            with contextlib.ExitStack() as es:
                P = newprog()
                psrc = dict(gk=p_gk, lr=p_lr, bv=p_bv, rope=ropep[d])
                gla_phase(es, P, psrc, 512, 8, d, False, (s_sctx[d], fA[d]), s_pre[d], s_mix=(s_sctx[d], f1[d], f2[d]))
                P.emit()
        for d in range(2):
            with contextlib.ExitStack() as es:
                P = newprog()
                gla_phase(es, P, msrc, 512, 8, d, True, (s_pre[d], one1), None)
                P.emit()
        if stop_after <= 4:
            return nc

        with contextlib.ExitStack() as es:
            P = newprog()
            bt2b = sb(es, "bt2b", [64, 16 * 8 * 128], BF16)
            mskb = sb(es, "mskb", [1, 8 * 6 * 128], BF16)
            onesb = sb(es, "onesb", [1, 64], BF16)
            qTt = sb(es, "qTt", [128, 8, 512], BF16)
            kband = sb(es, "kband", [128, 8, 960], BF16)
            vband = sb(es, "vband", [128, 14, 1040], BF16)
            PT = [sb(es, "PT%d" % i, [128, 512], BF16) for i in range(2)]
            azt = sb(es, "azt", [64, 1024], BF16)
            sza = sb(es, "sza", [64, 1024], F32)
            ya = sb(es, "ya", [64, 1024], F32)
            rcp = sb(es, "rcp", [64, 16], F32)
            zat = sb(es, "zat", [64, 1024], BF16)
            ST = [ps(es, "ST%d" % i, [128, 512], F32) for i in range(2)]
            OA = ps(es, "OA", [64, 3, 512], F32)
            P.dma("pool", lambda e: e.dma_start(out=bt2b[:], in_=bt2_d), "bt2b", writes=["bt2b"])
            P.dma("pool", lambda e: e.dma_start(out=mskb[:], in_=msk_d), "mskb", writes=["mskb"])
            P.op("dve", lambda e: e.memset(onesb[:], 1.0), writes=["onesb"])
            P.dma("sp", lambda e: e.dma_start(out=kTc[:], in_=c_nak.rearrange("m p t -> p m t")), "kTc", writes=["kTc"])
            P.dma("sp", lambda e: e.dma_start(out=vc[:], in_=c_nav.rearrange("(b p) n -> p b n", p=128)), "vc", writes=["vc"])
            bt2v = bt2b[:].rearrange("c (h q k) -> c h q k", h=16, q=8)
            mskv = mskb[:].rearrange("o (s j k) -> o s j k", s=8, j=6)
            scnt = 0
            for gi in range(8):
                e0 = gi * 8 * 64
                P.dma("sp", lambda e, gi=gi: e.dma_start(out=qTt[:], in_=s_naq[:, :, gi * 512:(gi + 1) * 512].rearrange("m p t -> p m t")),
                      "qTt", writes=["qTt"])
                P.dma("sp", lambda e, e0=e0: e.dma_start(out=kband[:], in_=s_nak[:, :, e0:e0 + 960].rearrange("m p t -> p m t")),
                      "kband", writes=["kband"])
                for bi in range(14):
                    P.dma("sp", lambda e, bi=bi, e0=e0: e.dma_start(out=vband[:, bi, :], in_=s_nav[e0 + bi * 64:e0 + bi * 64 + 128, :]),
                          ("vband", bi), writes=[("vband", bi)])
                for ri in range(8):
                    lr = gi * 8 + ri
                    if lr < 4:
                        pairs, si = list(range(2, 8)), lr
                    elif lr >= 60:
                        pairs, si = list(range(0, 6)), lr - 56
                    else:
                        pairs, si = list(range(2, 6)), None
                    npair = len(pairs)
                    nblk = npair + 2
                    P.dma("sp", lambda e, lr=lr: e.dma_start(out=azt[:], in_=s_az[lr * 64:(lr + 1) * 64, :]), "azt", writes=["azt"])
                    P.op("act", lambda e: e.activation(out=sza[:], in_=azt[:], func=AF.Silu), reads=["azt"], writes=["sza"])
                    for h in range(16):
                        m, pb = h // 2, 64 * (h % 2)
                        sti = scnt % 2
                        scnt += 1
                        qv = qTt[pb:pb + 64, m, ri * 64:(ri + 1) * 64]
                        for j, pi in enumerate(pairs):
                            bi = lr + 2 * pi - 8 - (8 * gi - 4)
                            off = bi * 64
                            P.op("pe", lambda e, sti=sti, j=j, m=m, pb=pb, off=off, qv=qv: e.matmul(
                                ST[sti][:, j * 64:(j + 1) * 64], lhsT=kband[pb:pb + 64, m, off:off + 128], rhs=qv, start=True, stop=False),
                                reads=["kband", "qTt"], writes=[("ST", sti)])
                            P.op("pe", lambda e, sti=sti, j=j, h=h, pi=pi: e.matmul(
                                ST[sti][:, j * 64:(j + 1) * 64], lhsT=bt2v[:, h, pi, :], rhs=identb[0:64, 0:64], start=False, stop=(si is None)),
                                reads=["bt2b", "identb"], writes=[("ST", sti)])
                            if si is not None:
                                P.op("pe", lambda e, sti=sti, j=j, si=si: e.matmul(
                                    ST[sti][:, j * 64:(j + 1) * 64], lhsT=mskv[:, si, j, :], rhs=onesb[:], start=False, stop=True),
                                    reads=["mskb", "onesb"], writes=[("ST", sti)])
                        for cb in range(2):
                            j = npair + cb
                            P.op("pe", lambda e, sti=sti, j=j, m=m, pb=pb, cb=cb, qv=qv: e.matmul(
                                ST[sti][:, j * 64:(j + 1) * 64], lhsT=kTc[pb:pb + 64, m, cb * 128:(cb + 1) * 128], rhs=qv, start=True, stop=True),
                                reads=["kTc", "qTt"], writes=[("ST", sti)])
                        P.op("act", lambda e, sti=sti, nblk=nblk: e.activation(out=PT[sti][:, 0:nblk * 64], in_=ST[sti][:, 0:nblk * 64], func=AF.Exp),
                             reads=[("ST", sti)], writes=[("PT", sti)])
                        ob, oo = h % 3, (h // 3) * 65
                        for j in range(nblk):
                            if j < npair:
                                bi = lr + 2 * pairs[j] - 8 - (8 * gi - 4)
                                rhs = vband[:, bi, h * 65:(h + 1) * 65]
                                rd = ("vband", bi)
                            else:
                                rhs = vc[:, j - npair, h * 65:(h + 1) * 65]
                                rd = "vc"
                            P.op("pe", lambda e, sti=sti, j=j, rhs=rhs, ob=ob, oo=oo, nblk=nblk: e.matmul(
                                OA[:, ob, oo:oo + 65], lhsT=PT[sti][:, j * 64:(j + 1) * 64], rhs=rhs, start=(j == 0), stop=(j == nblk - 1)),
                                reads=[("PT", sti), rd], writes=[("OA", ob)])
                        P.op("dve", lambda e, h=h, ob=ob, oo=oo: e.reciprocal(out=rcp[:, h:h + 1], in_=OA[:, ob, oo + 64:oo + 65]),
                             reads=[("OA", ob)], writes=[("rcp", h)])
                        P.op("dve", lambda e, h=h, ob=ob, oo=oo: e.tensor_scalar(
                            out=ya[:, h * 64:(h + 1) * 64], in0=OA[:, ob, oo:oo + 64], scalar1=rcp[:, h:h + 1], scalar2=None, op0=ALU.mult),
                            reads=[("OA", ob), ("rcp", h)], writes=[("ya", h)])
                    P.op("dve", lambda e: e.tensor_tensor(out=zat[:], in0=ya[:], in1=sza[:], op=ALU.mult),
                         reads=[("ya", h) for h in range(16)] + ["sza"], writes=["zat"])
                    P.dma("sp", lambda e, lr=lr: e.dma_start(out=s_za[lr * 64:(lr + 1) * 64, :], in_=zat[:]), "zat", reads=["zat"])
            P.emit()
        if stop_after <= 5:
            return nc

        with contextlib.ExitStack() as es:
            P = newprog()
            zin_a = [sb(es, "zin_a%d" % i, [128, 1024], BF16) for i in range(1)]
            zin_b = [sb(es, "zin_b%d" % i, [128, 2048], BF16) for i in range(1)]
            zaT = sb(es, "zaT", [128, 8, 512], BF16)
            zbT = sb(es, "zbT", [128, 16, 512], BF16)
            mgt = [sb(es, "mgt%d" % i, [128, 8, 512], BF16) for i in range(2)]
            mT = sb(es, "mT", [128, 16, 512], BF16)
            wa = sb(es, "wa", [128, 8, 512], BF16)
            wb = sb(es, "wb", [128, 16, 512], BF16)
            wo = sb(es, "wo", [128, 16, 512], BF16)
            sa = sb(es, "sa", [128, 512], F32)
            sbb = sb(es, "sbb", [128, 512], F32)
            m1 = sb(es, "m1", [128, 512], F32)
            m2 = sb(es, "m2", [128, 512], F32)
            xr = sb(es, "xr", [128, 2048], F32)
            ut = sb(es, "ut", [128, 4, 2048], F32)
            yo = sb(es, "yo", [128, 2048], F32)
            lng = sb(es, "lng", [128, 4096], F32)
            st = sb(es, "mst", [128, 4, 6], F32)
            mv = sb(es, "mmv", [128, 2], F32)
            rs = sb(es, "mrs", [128, 2], F32)
            tpz = [ps(es, "tpz%d" % i, [128, 4, 128], F32) for i in range(2)]
            ppa = ps(es, "ppa", [128, 512], F32)
            ppb = ps(es, "ppb", [128, 512], F32)
            pout = [ps(es, "pout%d" % i, [128, 512], F32) for i in range(2)]
            P.dma("sp", lambda e: e.dma_start(out=lng[:], in_=lngb), "lng", writes=["lng"])
            tcnt = 0
            ocnt = 0
            for t in range(8):
                tok0 = t * 512
                for s in range(4):
                    zi = 0
                    r0 = tok0 + s * 128
                    P.dma("sp", lambda e, zi=zi, r0=r0: e.dma_start(out=zin_a[zi][:], in_=s_za[r0:r0 + 128, :]), ("zin_a", zi), writes=[("zin_a", zi)])
                    P.dma("sp", lambda e, zi=zi, r0=r0: e.dma_start(out=zin_b[zi][:], in_=s_zb[r0:r0 + 128, :]), ("zin_b", zi), writes=[("zin_b", zi)])
                    for (zin, nm, nk, dstT, dn) in ((zin_a, "zin_a", 8, zaT, "zaT"), (zin_b, "zin_b", 16, zbT, "zbT")):
                        for kg in range(nk // 4):
                            ti = tcnt % 2
                            tcnt += 1
                            for kk in range(4):
                                k = kg * 4 + kk
                                P.op("pe", lambda e, zin=zin, zi=zi, k=k, kk=kk, ti=ti: e.matmul(
                                    tpz[ti][:, kk, :], lhsT=zin[zi][:, k * 128:(k + 1) * 128], rhs=identb[:], start=True, stop=True),
                                    reads=[(nm, zi), "identb"], writes=[("tpz", ti)])
                            if tcnt % 2 == 0:
                                P.op("act", lambda e, dstT=dstT, kg=kg, s=s, ti=ti: e.activation(
                                    out=dstT[:, kg * 4:(kg + 1) * 4, s * 128:(s + 1) * 128], in_=tpz[ti][:], func=AF.Copy),
                                    reads=[("tpz", ti)], writes=[(dn, s)])
                            else:
                                P.op("dve", lambda e, dstT=dstT, kg=kg, s=s, ti=ti: e.tensor_copy(
                                    out=dstT[:, kg * 4:(kg + 1) * 4, s * 128:(s + 1) * 128], in_=tpz[ti][:]),
                                    reads=[("tpz", ti)], writes=[(dn, s)])
                zaTr = [("zaT", s) for s in range(4)]
                zbTr = [("zbT", s) for s in range(4)]
                for fg in range(4):
                    c0 = fg * 512
                    gi2 = fg % 2
                    P.dma("pool", lambda e, c0=c0: e.dma_start(out=wa[:], in_=wbra[:, c0:c0 + 512].rearrange("(k p) n -> p k n", p=128)),
                          "wa", writes=["wa"])
                    P.dma("pool", lambda e, c0=c0: e.dma_start(out=wb[:], in_=wbrb[:, c0:c0 + 512].rearrange("(k p) n -> p k n", p=128)),
                          "wb", writes=["wb"])
                    P.dma("sp", lambda e, fg=fg, gi2=gi2, tok0=tok0: e.dma_start(
                        out=mgt[gi2][:, 0:4, :], in_=s_mg[fg * 4:fg * 4 + 4, :, tok0:tok0 + 512].rearrange("c p t -> p c t")),
                        ("mgta", gi2), writes=[("mgta", gi2)])
                    P.dma("sp", lambda e, fg=fg, gi2=gi2, tok0=tok0: e.dma_start(
                        out=mgt[gi2][:, 4:8, :], in_=s_mg[16 + fg * 4:16 + fg * 4 + 4, :, tok0:tok0 + 512].rearrange("c p t -> p c t")),
                        ("mgtb", gi2), writes=[("mgtb", gi2)])
                    for n in range(4):
                        fc = fg * 4 + n
                        for k in range(8):
                            P.op("pe", lambda e, k=k, n=n: e.matmul(ppa[:], lhsT=wa[:, k, n * 128:(n + 1) * 128], rhs=zaT[:, k, :],
                                                                   start=(k == 0), stop=(k == 7)),
                                 reads=["wa"] + zaTr, writes=["ppa"])
                        for k in range(16):
                            P.op("pe", lambda e, k=k, n=n: e.matmul(ppb[:], lhsT=wb[:, k, n * 128:(n + 1) * 128], rhs=zbT[:, k, :],
                                                                   start=(k == 0), stop=(k == 15)),
                                 reads=["wb"] + zbTr, writes=["ppb"])
                        P.op("act", lambda e, gi2=gi2, n=n: e.activation(out=sa[:], in_=mgt[gi2][:, n, :], func=AF.Sigmoid),
                             reads=[("mgta", gi2)], writes=["sa"])
                        P.op("act", lambda e, gi2=gi2, n=n: e.activation(out=sbb[:], in_=mgt[gi2][:, 4 + n, :], func=AF.Sigmoid),
                             reads=[("mgtb", gi2)], writes=["sbb"])
                        P.op("dve", lambda e: e.tensor_tensor(out=m1[:], in0=ppa[:], in1=sa[:], op=ALU.mult), reads=["ppa", "sa"], writes=["m1"])
                        P.op("dve", lambda e: e.tensor_tensor(out=m2[:], in0=ppb[:], in1=sbb[:], op=ALU.mult), reads=["ppb", "sbb"], writes=["m2"])
                        P.op("dve", lambda e, fc=fc: e.tensor_tensor(out=mT[:, fc, :], in0=m1[:], in1=m2[:], op=ALU.add),
                             reads=["m1", "m2"], writes=[("mT", fc)])
                mTr = [("mT", fc) for fc in range(16)]
                for ng in range(4):
                    c0 = ng * 512
                    P.dma("pool", lambda e, c0=c0: e.dma_start(out=wo[:], in_=wout[:, c0:c0 + 512].rearrange("(k p) n -> p k n", p=128)),
                          "wo", writes=["wo"])
                    for s in range(4):
                        oi = ocnt % 2
                        ocnt += 1
                        for k in range(16):
                            P.op("pe", lambda e, k=k, s=s, oi=oi: e.matmul(pout[oi][:], lhsT=mT[:, k, s * 128:(s + 1) * 128], rhs=wo[:, k, :],
                                                                          start=(k == 0), stop=(k == 15)),
                                 reads=["wo"] + mTr, writes=[("pout", oi)])
                        P.op("dve", lambda e, s=s, oi=oi, c0=c0: e.tensor_tensor(out=ut[:, s, c0:c0 + 512], in0=pout[oi][:], in1=gateb[:, c0:c0 + 512], op=ALU.mult),
                             reads=[("pout", oi), "gateb"], writes=[("ut", s)])
                for s in range(4):
                    r0 = tok0 + s * 128
                    P.dma("sp", lambda e, r0=r0: e.dma_start(out=xr[:], in_=xm[r0:r0 + 128, :]), "xr", writes=["xr"])
                    P.op("dve", lambda e, s=s: e.scalar_tensor_tensor(out=ut[:, s, :], in0=xr[:], scalar=float(2.0 ** 0.25), in1=ut[:, s, :],
                                                                      op0=ALU.mult, op1=ALU.add),
                         reads=["xr", ("ut", s)], writes=[("ut", s)])
                    for j in range(4):
                        P.op("dve", lambda e, s=s, j=j: e.bn_stats(out=st[:, j, :], in_=ut[:, s, j * 512:(j + 1) * 512]),
                             reads=[("ut", s)], writes=["mst"])
                    P.op("dve", lambda e: e.bn_aggr(out=mv[:], in_=st[:].rearrange("p a b -> p (a b)")), reads=["mst"], writes=["mmv"])
                    P.op("dve", lambda e: e.tensor_scalar(out=rs[:, 0:1], in0=mv[:, 1:2], scalar1=1e-6, scalar2=None, op0=ALU.add),
                         reads=["mmv"], writes=["mrs0"])
                    P.op("act", lambda e: e.activation(out=rs[:, 0:1], in_=rs[:, 0:1], func=AF.Sqrt), reads=["mrs0"], writes=["mrs0"])
                    P.op("dve", lambda e: e.reciprocal(out=rs[:, 1:2], in_=rs[:, 0:1]), reads=["mrs0"], writes=["mrs1"])
                    P.op("dve", lambda e, s=s: e.tensor_scalar(out=yo[:], in0=ut[:, s, :], scalar1=mv[:, 0:1], scalar2=rs[:, 1:2],
                                                               op0=ALU.subtract, op1=ALU.mult),
                         reads=[("ut", s), "mmv", "mrs1"], writes=["yo"])
                    P.op("dve", lambda e: e.tensor_tensor(out=yo[:], in0=yo[:], in1=lng[:, 0:2048], op=ALU.mult), reads=["yo", "lng"], writes=["yo"])
                    P.op("dve", lambda e: e.tensor_tensor(out=yo[:], in0=yo[:], in1=lng[:, 2048:4096], op=ALU.add), reads=["yo", "lng"], writes=["yo"])
                    P.dma("sp", lambda e, r0=r0: e.dma_start(out=yout[r0:r0 + 128, :], in_=yo[:]), "yo", reads=["yo"])
            P.emit()
    return nc


_NC_CACHE = {}


def _consts():
    cst = np.zeros((128, 2048), np.float32)
    cst[:, 0:128] = np.eye(128, dtype=np.float32)
    perm = np.zeros((128, 128), np.float32)
    for m in range(128):
        perm[(m + 64) % 128, m] = 1.0
    cst[:, 128:256] = perm
    j = np.arange(64)[:, None]
    i = np.arange(64)[None, :]
    cst[0:64, 256:512] = np.tile((j <= i).astype(np.float32), (1, 4))
    cst[0:64, 512:768] = np.tile((j >= i).astype(np.float32), (1, 4))
    r = np.ones((128, 512), np.float32)
    r[:, 0::64] = 0.0
    cst[:, 768:1280] = r
    cst[:, 1282] = LNQ
    cst[:, 1283] = 0.0
    cst[:, 1284] = 1.0
    cst[:, 1285] = 1e-6
    return cst


def _rope_tables(rows, cols):
    inv = (np.float32(10000.0) ** (-np.arange(64, dtype=np.float32) / np.float32(64))).astype(np.float32)
    out = np.zeros((128, 4, rows.shape[0]), np.float32)
    for a, pos in enumerate((rows, cols)):
        ang = (pos[None, :].astype(np.float32) * inv[:, None]).astype(np.float32)
        c = np.cos(ang).astype(np.float32)
        s = np.sin(ang).astype(np.float32)
        out[0:64, a] = c
        out[64:128, a] = c
        out[0:64, 2 + a] = -s
        out[64:128, 2 + a] = s
    return out


def _fm(v):
    return np.ascontiguousarray(v.reshape(16, 128).T)


def _host_prep(x, c, ctx, c_ctx, w_mod, b_mod, w_in, na_rpb, gla_w_gate2, gla_b_gate,
               gla_norm_g, w_br_a, w_br_b, w_out, ln_g, ln_b):
    f32 = np.float32
    x = np.asarray(x, f32); c = np.asarray(c, f32); ctx = np.asarray(ctx, f32); c_ctx = np.asarray(c_ctx, f32)
    w_mod = np.asarray(w_mod, f32)[0]; b_mod = np.asarray(b_mod, f32)[0]; w_in = np.asarray(w_in, f32)[0]
    rpb = np.asarray(na_rpb, f32)[0]; wg2 = np.asarray(gla_w_gate2, f32)[0]; bgt = np.asarray(gla_b_gate, f32)[0]
    ng = np.asarray(gla_norm_g, f32)[0]
    wbra = np.asarray(w_br_a, f32)[0]; wbrb = np.asarray(w_br_b, f32)[0]; wout = np.asarray(w_out, f32)[0]
    lng = np.asarray(ln_g, f32)[0]; lnb = np.asarray(ln_b, f32)[0]
    bmodT = np.concatenate([_fm(b_mod[0:2048]), _fm(b_mod[2048:4096])], axis=1)
    bgate = np.ascontiguousarray(np.broadcast_to(b_mod[4096:6144][None, :], (128, 2048)))
    w2d = np.zeros((17, 2, 8, 128), f32)
    for d in range(2):
        for hd in range(4):
            for a in range(2):
                cols = hd * 128 + a * 64 + np.concatenate([np.arange(64), np.arange(64)])
                w2d[0:16, d, hd * 2 + a, :] = wg2[d][:, cols]
                w2d[16, d, hd * 2 + a, :] = bgt[d][cols]
    w2d = w2d.reshape(17, 2048)
    cc = np.arange(64)
    cstart = np.clip(cc - 8, 0, 48)
    jj = np.arange(64)
    inwin = (jj[None, :] >= cstart[:, None]) & (jj[None, :] < cstart[:, None] + 16)
    dc = np.clip(jj[None, :] - cc[:, None] + 15, 0, 30)
    bt2 = np.full((64, 16, 8, 2, 64), NEG, f32)
    for pi in range(8):
        for w in range(2):
            dr = 2 * pi - 1 + w
            if dr < 0 or dr > 14:
                continue
            vals = rpb[:, dr, :][:, dc]
            vals = np.where(inwin[None], vals, f32(NEG))
            bt2[:, :, pi, w, :] = vals.transpose(1, 0, 2)
    bt2 = bt2.reshape(64, 16 * 8 * 128)
    ngrep = np.ascontiguousarray(np.broadcast_to(ng[None, :], (64, 512)))
    lngb = np.ascontiguousarray(np.concatenate([np.broadcast_to(lng[None, :], (128, 2048)),
                                                np.broadcast_to(lnb[None, :], (128, 2048))], axis=1))
    ropec = _rope_tables(np.zeros(256, f32), np.zeros(256, f32))[None]
    cst0 = _consts()
    in_maps = []
    for core in range(8):
        b, q = core // 4, core % 4
        xb = x[b]
        xm = np.ascontiguousarray(xb[q * 4096:(q + 1) * 4096])
        xh = np.zeros((512, 2048), f32)
        if q > 0:
            xh[0:256] = xb[q * 4096 - 256:q * 4096]
        if q < 3:
            xh[256:512] = xb[(q + 1) * 4096:(q + 1) * 4096 + 256]
        cT = np.zeros((128, 16, 2), f32)
        cT[:, :, 0] = _fm(c[b])
        cT[:, :, 1] = _fm(c_ctx)
        pos = np.arange(q * 4096, (q + 1) * 4096)
        rt = _rope_tables((pos // 64).astype(f32), (pos % 64).astype(f32))
        ropem = np.ascontiguousarray(rt.reshape(128, 4, 8, 512).transpose(2, 0, 1, 3))
        msk = np.zeros((8, 6, 2, 64), f32)
        for si in range(8):
            lr = si if si < 4 else 56 + si
            pairs = list(range(2, 8)) if si < 4 else list(range(0, 6))
            r = 64 * q + lr
            rstart = min(max(r - 4, 0), 248)
            for j, pi in enumerate(pairs):
                for w in range(2):
                    rr = r + (2 * pi - 1 + w) - 7
                    ok = (rstart <= rr < rstart + 8)
                    msk[si, j, w, :] = 0.0 if ok else NEG
        cst = cst0.copy()
        cst[:, 1280] = 1.0 if q == 0 else 0.0
        cst[:, 1281] = 1.0 if q == 3 else 0.0
        cst[:, 1286] = 1.0 if q == 1 else 0.0
        cst[:, 1287] = 1.0 if q >= 1 else 0.0
        cst[:, 1288] = 1.0 if q == 0 else 0.0
        cst[:, 1289] = 1.0 if q == 2 else 0.0
        cst[:, 1290] = 1.0 if q <= 2 else 0.0
        cst[:, 1291] = 1.0 if q == 3 else 0.0
        xpn = np.zeros((2, 4096, 2048), f32)
        ropep = np.zeros((2, 8, 128, 4, 512), f32)
        for dd, qq in ((0, q - 1), (1, q + 1)):
            qc = min(max(qq, 0), 3)
            pp = np.arange(qc * 4096, (qc + 1) * 4096)
            rtp = _rope_tables((pp // 64).astype(f32), (pp % 64).astype(f32))
            ropep[dd] = rtp.reshape(128, 4, 8, 512).transpose(2, 0, 1, 3)
            if 0 <= qq <= 3:
                xpn[dd] = xb[qq * 4096:(qq + 1) * 4096]
        in_maps.append(dict(
            xm=xm, xh=xh, xc=np.ascontiguousarray(ctx[b]), cT=cT.reshape(128, 32), wmod=w_mod, bmodT=bmodT, bgate=bgate,
            win=w_in, ropem=ropem, ropec=ropec, xp=xpn, ropep=ropep, w2d=w2d, bt2=bt2, msk=msk.reshape(1, -1), cst=cst, ngrep=ngrep,
            wbra=wbra, wbrb=wbrb, wout=wout, lngb=lngb))
    return in_maps


def kernel(**inputs):
    in_maps = _host_prep(**inputs)
    if "nc" not in _NC_CACHE:
        _NC_CACHE["nc"] = build()
    nc = _NC_CACHE["nc"]
    res = run_bass_kernel_spmd(nc, in_maps, core_ids=list(range(8)))
    out = np.zeros((2, 16384, 2048), np.float32)
    for core in range(8):
        b, q = core // 4, core % 4
        out[b, q * 4096:(q + 1) * 4096] = res.results[core]["yout"]
    return out
```

```python
import contextlib
import numpy as np
import concourse.bass as bass
import concourse.mybir as mybir
from concourse.bass_utils import run_bass_kernel_spmd

F32 = mybir.dt.float32
BF16 = mybir.dt.bfloat16
ALU = mybir.AluOpType
AF = mybir.ActivationFunctionType

ENGS = ("pe", "act", "dve", "pool", "sp")
NEG = -30000.0
LNQ = float(np.log(1.0 / 16.0))


class Prog:
    uid = 0

    def __init__(self, nc, bar, bar_base, G):
        self.nc = nc
        self.G = G
        self.ops = {e: [] for e in ENGS}
        self.last_w = {}
        self.readers = {}
        self.dma_cnt = {}
        self.bar = bar
        self.bar_base = bar_base

    def _deps(self, reads, writes):
        deps = []
        for t in reads:
            w = self.last_w.get(t)
            if w is not None:
                deps.append(w)
        for t in writes:
            w = self.last_w.get(t)
            if w is not None:
                deps.append(w)
            deps.extend(self.readers.get(t, ()))
        return deps

    def _record(self, ev, reads, writes):
        for t in reads:
            self.readers.setdefault(t, []).append(ev)
        for t in writes:
            self.last_w[t] = ev
            self.readers[t] = []

    pe_nowait = False

    def op(self, eng, fn, reads=(), writes=()):
        deps = self._deps(reads, writes)
        if eng == "pe" and self.pe_nowait:
            deps = [d for d in deps if not (d[0] == "E" and d[1] == "pe")]
        ev = ("E", eng, len(self.ops[eng]))
        self.ops[eng].append(dict(fn=fn, deps=deps, dma=None, sig=False))
        self._record(ev, reads, writes)
        return ev

    def dma(self, eng, fn, semkey, reads=(), writes=()):
        deps = self._deps(reads, writes)
        n = self.dma_cnt.get(semkey, 0) + 1
        self.dma_cnt[semkey] = n
        ev = ("D", semkey, 16 * n)
        self.ops[eng].append(dict(fn=fn, deps=deps, dma=semkey, sig=False))
        self._record(ev, reads, writes)
        return ev

    def emit(self):
        nc = self.nc
        G = self.G
        for e in ENGS:
            for o in self.ops[e]:
                for d in o["deps"]:
                    if d[0] == "E":
                        self.ops[d[1]][d[2]]["sig"] = True
            for o in reversed(self.ops[e]):
                if o["dma"] is None:
                    o["sig"] = True
                    break
        sigval = {}
        for e in ENGS:
            c = G["ebase"][e]
            arr = []
            for o in self.ops[e]:
                if o["sig"]:
                    c += 1
                arr.append(c)
            sigval[e] = arr
        esem = G["esem"]
        dmap = {}
        for i, k in enumerate(self.dma_cnt):
            assert i < len(G["dsem"]), "dma semaphore pool too small"
            dmap[k] = i
        dsem = {k: G["dsem"][i] for k, i in dmap.items()}
        dbase = {k: G["dbase"][i] for k, i in dmap.items()}
        with contextlib.ExitStack() as es:
            block = es.enter_context(nc.Block())

            def resolve(d):
                if d[0] == "E":
                    return (esem[d[1]], sigval[d[1]][d[2]], ("E", d[1]))
                return (dsem[d[1]], dbase[d[1]] + d[2], ("D", d[1]))

            def run(e, engobj):
                waited = {}
                for o in self.ops[e]:
                    need = {}
                    for d in o["deps"]:
                        sem, val, key = resolve(d)
                        if waited.get(key, 0) >= val:
                            continue
                        if need.get(key, (None, 0))[1] < val:
                            need[key] = (sem, val)
                    for key, (sem, val) in need.items():
                        engobj.wait_ge(sem, val)
                        waited[key] = val
                    inst = o["fn"](engobj)
                    if o["dma"] is not None:
                        inst.then_inc(dsem[o["dma"]], 16)
                    elif o["sig"]:
                        inst.then_inc(esem[e], 1)
                if sigval[e] and sigval[e][-1] > G["ebase"][e]:
                    engobj.wait_ge(esem[e], sigval[e][-1])
                if e == "sp":
                    for k, n in self.dma_cnt.items():
                        engobj.wait_ge(dsem[k], dbase[k] + 16 * n)
                engobj.sem_inc(self.bar, 1)
                engobj.wait_ge(self.bar, self.bar_base + len(ENGS))

            @block.tensor
            def _(eng):
                run("pe", eng)

            @block.scalar
            def _(eng):
                run("act", eng)

            @block.vector
            def _(eng):
                run("dve", eng)

            @block.gpsimd
            def _(eng):
                run("pool", eng)

            @block.sync
            def _(eng):
                run("sp", eng)
        for e in ENGS:
            if sigval[e]:
                G["ebase"][e] = sigval[e][-1]
        for k, i in dmap.items():
            G["dbase"][i] += 16 * self.dma_cnt[k]


C_AQ, C_AK, C_AV, C_AZ, C_BQ, C_BK, C_BV, C_BZ, C_BG, C_MGA, C_MGB = (
    0, 1024, 2048, 3072, 4096, 5120, 6144, 8192, 10240, 10272, 12320)


def build(stop_after=99, debug=()):
    nc = bass.Bass("TRN2", target_bir_lowering=False)
    din = lambda name, shape, dt=F32: nc.dram_tensor(name, list(shape), dt, kind="ExternalInput").ap()
    scr = lambda name, shape, dt=BF16: nc.dram_tensor(
        name, list(shape), dt, kind=("ExternalOutput" if name in debug else "Internal")).ap()

    xm = din("xm", [4096, 2048])
    xh = din("xh", [512, 2048])
    xc = din("xc", [256, 2048])
    cT = din("cT", [128, 32])
    wmod = din("wmod", [2048, 6144])
    bmodT = din("bmodT", [128, 32])
    bgate = din("bgate", [128, 2048])
    win = din("win", [2048, 14368])
    ropem = din("ropem", [8, 128, 4, 512])
    ropec = din("ropec", [1, 128, 4, 256])
    xp = din("xp", [2, 4096, 2048])
    ropep = din("ropep", [2, 8, 128, 4, 512])
    w2d_d = din("w2d", [17, 2048])
    bt2_d = din("bt2", [64, 16 * 8 * 128])
    msk_d = din("msk", [1, 8 * 6 * 128])
    cst_d = din("cst", [128, 2048])
    ngrep = din("ngrep", [64, 512])
    wbra = din("wbra", [1024, 2048])
    wbrb = din("wbrb", [2048, 2048])
    wout = din("wout", [2048, 2048])
    lngb = din("lngb", [128, 4096])
    yout = nc.dram_tensor("yout", [4096, 2048], F32, kind="ExternalOutput").ap()

    s_naq = scr("s_naq", [8, 128, 4096])
    s_nak = scr("s_nak", [8, 128, 4608])
    s_nav = scr("s_nav", [4608, 1040])
    s_az = scr("s_az", [4096, 1024])
    s_bz = scr("s_bz", [4096, 2048])
    s_bv = scr("s_bv", [4096, 2048])
    s_gq = scr("s_gq", [8, 128, 4096])
    s_gk = scr("s_gk", [8, 128, 4096])
    s_lr = scr("s_lr", [2, 16, 4096], F32)
    s_mg = scr("s_mg", [32, 128, 4096])
    s_of = scr("s_of", [4096, 2048], F32)
    s_zb = scr("s_zb", [4096, 2048])
    s_za = scr("s_za", [4096, 1024])
    c_nak = scr("c_nak", [8, 128, 256])
    c_nav = scr("c_nav", [256, 1040])
    c_gk = scr("c_gk", [8, 128, 256])
    c_bv = scr("c_bv", [256, 2048])
    c_lr = scr("c_lr", [2, 16, 256], F32)
    s_sctx = scr("s_sctx", [2, 128, 8 * 512], F32)
    s_pre = scr("s_pre", [2, 128, 8 * 512], F32)
    p_gk = scr("p_gk", [8, 128, 4096])
    p_bv = scr("p_bv", [4096, 2048])
    p_lr = scr("p_lr", [2, 16, 4096], F32)

    phase = [0]
    with contextlib.ExitStack() as top:
        uid = [0]

        def sb(es, name, shape, dt):
            uid[0] += 1
            return es.enter_context(nc.sbuf_tensor("%s_u%d" % (name, uid[0]), list(shape), dt))

        def ps(es, name, shape, dt):
            uid[0] += 1
            return es.enter_context(nc.psum_tensor("%s_u%d" % (name, uid[0]), list(shape), dt))
        bar = top.enter_context(nc.semaphore("phasebar"))
        G = dict(esem={e: top.enter_context(nc.semaphore("esem_" + e)) for e in ENGS},
                 ebase={e: 0 for e in ENGS},
                 dsem=[top.enter_context(nc.semaphore("dsem_%d" % i)) for i in range(48)],
                 dbase=[0] * 48)

        def newprog(pe_nowait=False):
            p = Prog(nc, bar, phase[0] * len(ENGS), G)
            p.pe_nowait = pe_nowait
            phase[0] += 1
            return p

        cst = sb(top, "cst", [128, 2048], F32)
        identb = sb(top, "identb", [128, 128], BF16)
        permb = sb(top, "permb", [128, 128], BF16)
        sc1 = sb(top, "sc1", [128, 32], F32)
        sh = sb(top, "sh", [128, 32], F32)
        gateb = sb(top, "gateb", [128, 2048], F32)
        w2d = sb(top, "w2d", [17, 2048], F32)
        kTc = sb(top, "kTc", [128, 8, 256], BF16)
        vc = sb(top, "vc", [128, 2, 1040], BF16)
        ident_f = cst[:, 0:128]
        perm_f = cst[:, 128:256]
        maskF = cst[0:64, 256:512]
        maskB = cst[0:64, 512:768]
        rstT = cst[:, 768:1280]
        usef = cst[:, 1280:1281]
        useb = cst[:, 1281:1282]
        lnq = cst[:, 1282:1283]
        zero1 = cst[:, 1283:1284]
        one1 = cst[:, 1284:1285]
        eps6 = cst[:, 1285:1286]
        fA = (cst[:, 1286:1287], cst[:, 1289:1290])
        f1 = (cst[:, 1287:1288], cst[:, 1290:1291])
        f2 = (cst[:, 1288:1289], cst[:, 1291:1292])

        with contextlib.ExitStack() as es:
            P = newprog()
            cTt = sb(es, "cTt", [128, 32], F32)
            scs = sb(es, "scs", [128, 32], F32)
            screp = sb(es, "screp", [128, 16, 128], F32)
            bmt = sb(es, "bmt", [128, 32], F32)
            wm = [sb(es, "wm%d" % i, [128, 16, 512], F32) for i in range(2)]
            pm = ps(es, "pm", [128, 64], F32)
            pg = ps(es, "pg", [128, 512], F32)
            P.dma("sp", lambda e: e.dma_start(out=cst[:], in_=cst_d), "cst", writes=["cst"])
            P.dma("sp", lambda e: e.dma_start(out=cTt[:], in_=cT), "cTt", writes=["cTt"])
            P.dma("sp", lambda e: e.dma_start(out=bmt[:], in_=bmodT), "bmt", writes=["bmt"])
            P.dma("sp", lambda e: e.dma_start(out=w2d[:], in_=w2d_d), "w2d", writes=["w2d"])
            P.dma("sp", lambda e: e.dma_start(out=gateb[:], in_=bgate), "gateb", writes=["gateb"])
            P.op("dve", lambda e: e.tensor_copy(out=identb[:], in_=ident_f), reads=["cst"], writes=["identb"])
            P.op("dve", lambda e: e.tensor_copy(out=permb[:], in_=perm_f), reads=["cst"], writes=["permb"])
            P.op("act", lambda e: e.activation(out=scs[:], in_=cTt[:], func=AF.Silu), reads=["cTt"], writes=["scs"])
            scs3 = scs[:].rearrange("p (k j) -> p k j", j=2)
            P.op("dve", lambda e: e.tensor_copy(out=screp[:], in_=scs3[:, :, 0:1].broadcast_to([128, 16, 128])),
                 reads=["scs"], writes=["screp"])
            for g in range(12):
                buf = wm[g % 2]
                bt = ("wm", g % 2)
                P.dma("sp", lambda e, g=g, buf=buf: e.dma_start(
                    out=buf[:], in_=wmod[:, g * 512:(g + 1) * 512].rearrange("(k p) n -> p k n", p=128)),
                    bt, writes=[bt])
                if g < 8:
                    for n in range(4):
                        ch = g * 4 + n
                        for k in range(16):
                            P.op("pe", lambda e, buf=buf, n=n, k=k, ch=ch: e.matmul(
                                pm[:, ch * 2:ch * 2 + 2], lhsT=buf[:, k, n * 128:(n + 1) * 128],
                                rhs=scs[:, k * 2:k * 2 + 2], start=(k == 0), stop=(k == 15)),
                                reads=[bt, "scs"], writes=["pm"])
                else:
                    for k in range(16):
                        P.op("pe", lambda e, buf=buf, k=k: e.matmul(
                            pg[:], lhsT=screp[:, k, :], rhs=buf[:, k, :], start=(k == 0), stop=(k == 15)),
                            reads=[bt, "screp"], writes=["pg"])
                    c0 = (g - 8) * 512
                    P.op("dve", lambda e, c0=c0: e.tensor_tensor(
                        out=gateb[:, c0:c0 + 512], in0=pg[:], in1=gateb[:, c0:c0 + 512], op=ALU.add),
                        reads=["pg", "gateb"], writes=["gateb"])
            pm3 = pm[:].rearrange("p (c j) -> p c j", j=2)
            sh3 = sh[:].rearrange("p (k j) -> p k j", j=2)
            sc3 = sc1[:].rearrange("p (k j) -> p k j", j=2)
            for j in range(2):
                P.op("dve", lambda e, j=j: e.tensor_tensor(out=sh3[:, :, j], in0=pm3[:, 0:16, j], in1=bmt[:, 0:16], op=ALU.add),
                     reads=["pm", "bmt"], writes=["sh"])
                P.op("dve", lambda e, j=j: e.scalar_tensor_tensor(out=sc3[:, :, j], in0=pm3[:, 16:32, j], scalar=1.0,
                                                                  in1=bmt[:, 16:32], op0=ALU.add, op1=ALU.add),
                     reads=["pm", "bmt"], writes=["sc1"])
            if "dbgA" in debug:
                dbgA = nc.dram_tensor("dbgA", [128, 2112], F32, kind="ExternalOutput").ap()
                P.dma("sp", lambda e: e.dma_start(out=dbgA[:, 0:2048], in_=gateb[:]), "dbg0", reads=["gateb"])
                P.dma("sp", lambda e: e.dma_start(out=dbgA[:, 2048:2080], in_=sc1[:]), "dbg1", reads=["sc1"])
                P.dma("sp", lambda e: e.dma_start(out=dbgA[:, 2080:2112], in_=sh[:]), "dbg2", reads=["sh"])
            P.emit()
        if stop_after <= 0:
            return nc

        def proj_phase(es, P, xsrc, ntok, modj, fm_specs, tm_specs, lr_dst):
            xt = [sb(es, "xt%d" % i, [128, 2048], F32) for i in range(2)]
            xn = [sb(es, "xn%d" % i, [128, 2048], BF16) for i in range(2)]
            st = sb(es, "st", [128, 4, 6], F32)
            mv = sb(es, "mv", [128, 2], F32)
            rs = sb(es, "rs", [128, 2], F32)
            hT = sb(es, "hT", [128, 16, 512], BF16)
            wt = [sb(es, "wt%d" % i, [128, 16, 512], BF16) for i in range(2)]
            wg = sb(es, "wg", [128, 16, 32], BF16)
            stg = [sb(es, "stg%d" % i, [128, 512], BF16) for i in range(3)]
            stv = [sb(es, "stv%d" % i, [128, 8, 65], BF16) for i in range(2)]
            stl = sb(es, "stl", [16, 2, 512], F32)
            tp = [ps(es, "tp%d" % i, [128, 4, 128], F32) for i in range(2)]
            acc = [ps(es, "acc%d" % i, [128, 512], F32) for i in range(3)]
            accl = ps(es, "accl", [16, 2, 512], F32)
            for i in range(2):
                P.op("dve", lambda e, i=i: e.memset(stv[i][:], 1.0), writes=[("stv", i)])
            cnt = dict(w=0, stg=0, stv=0, acc=0, x=0, ev=0)
            ntile = (ntok + 511) // 512
            for t in range(ntile):
                tok0 = t * 512
                T = min(512, ntok - tok0)
                nsub = T // 128
                import os as _os2
                _k2 = _os2.environ.get("KDBG2", "")
                if _k2 == "skipln":
                    P.op("dve", lambda e: e.memset(hT[:], 0.5), writes=[("hT", s) for s in range(nsub)])
                for s in (range(nsub) if _k2 != "skipln" else []):
                    xi = cnt["x"] % 2
                    cnt["x"] += 1
                    r0 = tok0 + s * 128
                    P.dma("sp", lambda e, xi=xi, r0=r0: e.dma_start(out=xt[xi][:], in_=xsrc[r0:r0 + 128, :]),
                          ("xt", xi), writes=[("xt", xi)])
                    for j in range(4):
                        P.op("dve", lambda e, xi=xi, j=j: e.bn_stats(out=st[:, j, :], in_=xt[xi][:, j * 512:(j + 1) * 512]),
                             reads=[("xt", xi)], writes=["st"])
                    P.op("dve", lambda e: e.bn_aggr(out=mv[:], in_=st[:].rearrange("p a b -> p (a b)")),
                         reads=["st"], writes=["mv"])
                    P.op("dve", lambda e: e.tensor_scalar(out=rs[:, 0:1], in0=mv[:, 1:2], scalar1=1e-6, scalar2=None, op0=ALU.add),
                         reads=["mv"], writes=["rs0"])
                    P.op("act", lambda e: e.activation(out=rs[:, 0:1], in_=rs[:, 0:1], func=AF.Sqrt), reads=["rs0"], writes=["rs0"])
                    P.op("dve", lambda e: e.reciprocal(out=rs[:, 1:2], in_=rs[:, 0:1]), reads=["rs0"], writes=["rs1"])
                    P.op("dve", lambda e, xi=xi: e.tensor_scalar(out=xn[xi][:], in0=xt[xi][:], scalar1=mv[:, 0:1], scalar2=rs[:, 1:2],
                                                                op0=ALU.subtract, op1=ALU.mult),
                         reads=[("xt", xi), "mv", "rs1"], writes=[("xn", xi)])
                    for kq in range(4):
                        tb = kq % 2
                        for kk in range(4):
                            k = kq * 4 + kk
                            P.op("pe", lambda e, xi=xi, k=k, kk=kk, tb=tb: e.matmul(
                                tp[tb][:, kk, :], lhsT=xn[xi][:, k * 128:(k + 1) * 128], rhs=identb[:], start=True, stop=True),
                                reads=[("xn", xi), "identb"], writes=[("tp", tb)])
                        for kk in range(4):
                            k = kq * 4 + kk
                            P.op("act", lambda e, k=k, kk=kk, tb=tb, s=s: e.activation(
                                out=hT[:, k, s * 128:(s + 1) * 128], in_=tp[tb][:, kk, :], func=AF.Identity,
                                scale=sc1[:, 2 * k + modj:2 * k + modj + 1], bias=sh[:, 2 * k + modj:2 * k + modj + 1]),
                                reads=[("tp", tb), "sc1", "sh"], writes=[("hT", s)])
                hTr = [("hT", s) for s in range(nsub)]
                import os as _os
                _kd = _os.environ.get("KDBG", "fm,tm,lr").split(",")
                if "fm" not in _kd:
                    fm_specs = []
                if "tm" not in _kd:
                    tm_specs = []
                if "lr" not in _kd:
                    lr_dst = None
                for (col0, ng, dst, dch0, dtok0, scale) in fm_specs:
                    if dst is None:
                        continue
                    for g in range(ng):
                        wi = cnt["w"] % 2
                        cnt["w"] += 1
                        c0 = col0 + g * 512
                        P.dma("pool", lambda e, wi=wi, c0=c0: e.dma_start(
                            out=wt[wi][:], in_=win[:, c0:c0 + 512].rearrange("(k p) n -> p k n", p=128)),
                            ("wt", wi), writes=[("wt", wi)])
                        for n in range(4):
                            ai = cnt["acc"] % 3
                            cnt["acc"] += 1
                            for k in range(16):
                                P.op("pe", lambda e, wi=wi, n=n, k=k, ai=ai, T=T: e.matmul(
                                    acc[ai][:, 0:T], lhsT=wt[wi][:, k, n * 128:(n + 1) * 128], rhs=hT[:, k, 0:T],
                                    start=(k == 0), stop=(k == 15)),
                                    reads=[("wt", wi)] + hTr, writes=[("acc", ai)])
                            si = cnt["stg"] % 3
                            cnt["stg"] += 1
                            cnt["ev"] += 1
                            if cnt["ev"] % 2 == 0:
                                P.op("act", lambda e, si=si, ai=ai, T=T, scale=scale: e.activation(
                                    out=stg[si][:, 0:T], in_=acc[ai][:, 0:T], func=AF.Copy, scale=float(scale)),
                                    reads=[("acc", ai)], writes=[("stg", si)])
                            else:
                                P.op("dve", lambda e, si=si, ai=ai, T=T, scale=scale: e.tensor_scalar(
                                    out=stg[si][:, 0:T], in0=acc[ai][:, 0:T], scalar1=float(scale), scalar2=None, op0=ALU.mult),
                                    reads=[("acc", ai)], writes=[("stg", si)])
                            ch = dch0 + g * 4 + n
                            P.dma("sp", lambda e, si=si, ch=ch, T=T, dst=dst, a=dtok0 + tok0: e.dma_start(
                                out=dst[ch, :, a:a + T], in_=stg[si][:, 0:T]),
                                ("stg", si), reads=[("stg", si)])
                if lr_dst is not None:
                    P.dma("pool", lambda e: e.dma_start(
                        out=wg[:], in_=win[:, C_BG:C_BG + 32].rearrange("(k p) n -> p k n", p=128)),
                        "wg", writes=["wg"])
                    for d in range(2):
                        for k in range(16):
                            P.op("pe", lambda e, d=d, k=k, T=T: e.matmul(
                                accl[:, d, 0:T], lhsT=wg[:, k, d * 16:(d + 1) * 16], rhs=hT[:, k, 0:T],
                                start=(k == 0), stop=(k == 15)),
                                reads=["wg"] + hTr, writes=["accl"])
                    P.op("dve", lambda e, T=T: e.tensor_copy(out=stl[:, :, 0:T], in_=accl[:, :, 0:T]), reads=["accl"], writes=["stl"])
                    P.dma("sp", lambda e, T=T, a=tok0: e.dma_start(
                        out=lr_dst[:, :, a:a + T].rearrange("d r t -> r d t"), in_=stl[:, :, 0:T]),
                        "stl", reads=["stl"])
                for (col0, ng, dst, dcol0, dtok0, kind) in tm_specs:
                    if dst is None:
                        continue
                    for g in range(ng):
                        wi = cnt["w"] % 2
                        cnt["w"] += 1
                        c0 = col0 + g * 512
                        P.dma("pool", lambda e, wi=wi, c0=c0: e.dma_start(
                            out=wt[wi][:], in_=win[:, c0:c0 + 512].rearrange("(k p) n -> p k n", p=128)),
                            ("wt", wi), writes=[("wt", wi)])
                        for s in range(nsub):
                            ai = cnt["acc"] % 3
                            cnt["acc"] += 1
                            for k in range(16):
                                P.op("pe", lambda e, wi=wi, s=s, k=k, ai=ai: e.matmul(
                                    acc[ai][:], lhsT=hT[:, k, s * 128:(s + 1) * 128], rhs=wt[wi][:, k, :],
                                    start=(k == 0), stop=(k == 15)),
                                    reads=[("wt", wi), ("hT", s)], writes=[("acc", ai)])
                            a = dtok0 + tok0 + s * 128
                            cnt["ev"] += 1
                            eng = "act" if cnt["ev"] % 2 == 0 else "dve"
                            if kind == "v65":
                                vi = cnt["stv"] % 2
                                cnt["stv"] += 1
                                src = acc[ai][:].rearrange("p (h d) -> p h d", d=64)
                                if eng == "act":
                                    P.op("act", lambda e, vi=vi, src=src: e.activation(out=stv[vi][:, :, 0:64], in_=src, func=AF.Copy),
                                         reads=[("acc", ai)], writes=[("stv", vi)])
                                else:
                                    P.op("dve", lambda e, vi=vi, src=src: e.tensor_copy(out=stv[vi][:, :, 0:64], in_=src),
                                         reads=[("acc", ai)], writes=[("stv", vi)])
                                h0 = (dcol0 + g * 512) // 64
                                P.dma("sp", lambda e, vi=vi, a=a, h0=h0, dst=dst: e.dma_start(
                                    out=dst[a:a + 128, h0 * 65:(h0 + 8) * 65], in_=stv[vi][:].rearrange("p h d -> p (h d)")),
                                    ("stv", vi), reads=[("stv", vi)])
                            else:
                                si = cnt["stg"] % 3
                                cnt["stg"] += 1
                                if eng == "act":
                                    P.op("act", lambda e, si=si, ai=ai: e.activation(out=stg[si][:], in_=acc[ai][:], func=AF.Copy),
                                         reads=[("acc", ai)], writes=[("stg", si)])
                                else:
                                    P.op("dve", lambda e, si=si, ai=ai: e.tensor_copy(out=stg[si][:], in_=acc[ai][:]),
                                         reads=[("acc", ai)], writes=[("stg", si)])
                                cc = dcol0 + g * 512
                                P.dma("sp", lambda e, si=si, a=a, cc=cc, dst=dst: e.dma_start(
                                    out=dst[a:a + 128, cc:cc + 512], in_=stg[si][:]),
                                    ("stg", si), reads=[("stg", si)])

        with contextlib.ExitStack() as es:
            P = newprog(pe_nowait=True)
            proj_phase(es, P, xc, 256, 1,
                       fm_specs=[(C_AK, 2, c_nak, 0, 0, 1.0), (C_BK, 2, c_gk, 0, 0, 1.0)],
                       tm_specs=[(C_AV, 2, c_nav, 0, 0, "v65"), (C_BV, 4, c_bv, 0, 0, "plain")],
                       lr_dst=c_lr)
            P.emit()
        if stop_after <= 1:
            return nc

        with contextlib.ExitStack() as es:
            P = newprog(pe_nowait=True)
            proj_phase(es, P, xm, 4096, 0,
                       fm_specs=[(C_AQ, 2, s_naq, 0, 0, 0.125), (C_AK, 2, s_nak, 0, 256, 1.0),
                                 (C_BQ, 2, s_gq, 0, 0, 1.0), (C_BK, 2, s_gk, 0, 0, 1.0),
                                 (C_MGA, 4, s_mg, 0, 0, 1.0), (C_MGB, 4, s_mg, 16, 0, 1.0)],
                       tm_specs=[(C_AV, 2, s_nav, 0, 256, "v65"), (C_AZ, 2, s_az, 0, 0, "plain"),
                                 (C_BV, 4, s_bv, 0, 0, "plain"), (C_BZ, 4, s_bz, 0, 0, "plain")],
                       lr_dst=s_lr)
            P.emit()
        for part, (src_lo, ext_lo) in enumerate(((0, 0), (256, 4352))):
            with contextlib.ExitStack() as es:
                P = newprog(pe_nowait=True)
                proj_phase(es, P, xh[src_lo:src_lo + 256, :], 256, 0,
                           fm_specs=[(C_AK, 2, s_nak, 0, ext_lo, 1.0)],
                           tm_specs=[(C_AV, 2, s_nav, 0, ext_lo, "v65")],
                           lr_dst=None)
                P.emit()
        if stop_after <= 2:
            return nc

        def gla_phase(es, P, src, T, ntiles, d, outputs, s_init, s_final, s_mix=None):
            nch = T // 64
            gkt = sb(es, "gkt", [128, 8, 512], BF16)
            gqt = sb(es, "gqt", [128, 8, 512], BF16) if outputs else None
            ropet = sb(es, "ropet", [128, 4, 512], F32)
            lrt = sb(es, "lrt", [17, 512], F32)
            spt = [sb(es, "spt%d" % i, [128, 512], F32) for i in range(2)]
            ct = [sb(es, "ct%d" % i, [128, 512], F32) for i in range(2)]
            cl = sb(es, "cl", [128, 8, 8], F32)
            nb = sb(es, "nb", [128, 8, 8], F32)
            pbs = sb(es, "pbs", [128, 8, 8], F32)
            nbq = sb(es, "nbq", [128, 8, 8], F32)
            el = sb(es, "el", [128, 8, 8], F32)
            Eq = sb(es, "Eq", [128, 512], F32)
            Ek = sb(es, "Ek", [128, 512], F32)
            El = sb(es, "El", [128, 512], F32)
            t1 = [sb(es, "t1_%d" % i, [128, 512], F32) for i in range(2)]
            t2 = [sb(es, "t2_%d" % i, [128, 512], F32) for i in range(2)]
            rk = [sb(es, "rk_%d" % i, [128, 512], F32) for i in range(2)]
            qe = sb(es, "qe", [128, 8, 512], BF16) if outputs else None
            ke = sb(es, "ke", [128, 8, 512], BF16) if outputs else None
            kl = sb(es, "kl", [128, 8, 512], BF16)
            klT = sb(es, "klT", [64, 8, 8, 128], BF16)
            vt = [sb(es, "vt%d" % i, [64, 2048], BF16) for i in range(2)]
            S = sb(es, "S", [128, 8, 512], F32)
            Sbf = sb(es, "Sbf", [128, 8, 512], BF16) if outputs else None
            pz = ps(es, "pz", [128, 512], F32)
            pr = ps(es, "pr", [128, 512], F32)
            ptk = ps(es, "ptk", [64, 4, 128], F32)
            if outputs:
                attS = sb(es, "attS", [64, 4, 64], BF16)
                ost = sb(es, "ost", [64, 2048], F32)
                pa = ps(es, "pa", [64, 4, 64], F32)
                po = [ps(es, "po%d" % i, [64, 512], F32) for i in range(4)]
                if d == 1:
                    oft = sb(es, "oft", [64, 2048], F32)
                    bzt = sb(es, "bzt", [64, 2048], BF16)
                    sz = sb(es, "sz", [64, 2048], F32)
                    junk = sb(es, "junk", [64, 512], F32)
                    ssq = sb(es, "ssq", [64, 4], F32)
                    rr = sb(es, "rr", [64, 4], F32)
                    ngt = sb(es, "ngt", [64, 512], F32)
                    zbt = sb(es, "zbt", [64, 2048], BF16)
                    P.dma("sp", lambda e: e.dma_start(out=ngt[:], in_=ngrep), "ngt", writes=["ngt"])
            mask = (maskF if d == 0 else maskB).rearrange("p (h i) -> p h i", i=64)
            if s_init is None:
                P.op("pool", lambda e: e.memset(S[:], 0.0), writes=[("S", g) for g in range(8)])
            else:
                sap, use = s_init
                P.dma("sp", lambda e: e.dma_start(out=S[:].rearrange("p g n -> p (g n)"), in_=sap), "Sld",
                      writes=[("S", g) for g in range(8)])
                P.op("dve", lambda e: e.tensor_scalar(out=S[:], in0=S[:], scalar1=use, scalar2=None, op0=ALU.mult),
                     reads=[("S", g) for g in range(8)] + ["cst"], writes=[("S", g) for g in range(8)])
            if outputs:
                P.op("pool", lambda e: e.tensor_copy(out=Sbf[:], in_=S[:]), reads=[("S", g) for g in range(8)],
                     writes=[("Sbf", g) for g in range(8)])
            P.op("pool", lambda e: e.memset(lrt[:], 1.0), writes=["lrt"])
            tiles = list(range(ntiles))
            chunks = list(range(nch))
            if d == 1:
                tiles.reverse()
                chunks.reverse()
            sgn = 1.0 if d == 0 else -1.0
            vcnt = 0
            for t in tiles:
                tok0 = t * T
                P.dma("sp", lambda e, tok0=tok0: e.dma_start(out=gkt[:, :, 0:T], in_=src["gk"][:, :, tok0:tok0 + T].rearrange("g p t -> p g t")),
                      "gkt", writes=["gkt"])
                if outputs:
                    P.dma("sp", lambda e, tok0=tok0: e.dma_start(out=gqt[:, :, 0:T], in_=src["gq"][:, :, tok0:tok0 + T].rearrange("g p t -> p g t")),
                          "gqt", writes=["gqt"])
                P.dma("sp", lambda e, t=t: e.dma_start(out=ropet[:, :, 0:T], in_=src["rope"][t]), "ropet", writes=["ropet"])
                P.dma("sp", lambda e, tok0=tok0: e.dma_start(out=lrt[0:16, 0:T], in_=src["lr"][d, :, tok0:tok0 + T]), "lrt", writes=["lrt"])
                for g in range(8):
                    a = g % 2
                    i2 = g % 2
                    wcol = (d * 8 + g) * 128
                    P.op("pe", lambda e, wcol=wcol: e.matmul(pz[:, 0:T], lhsT=w2d[:, wcol:wcol + 128], rhs=lrt[:, 0:T], start=True, stop=True),
                         reads=["w2d", "lrt"], writes=["pz"])
                    P.op("act", lambda e, i2=i2: e.activation(out=spt[i2][:, 0:T], in_=pz[:, 0:T], func=AF.Exp, scale=-1.0),
                         reads=["pz"], writes=[("spt", i2)])
                    P.op("act", lambda e, i2=i2: e.activation(out=spt[i2][:, 0:T], in_=spt[i2][:, 0:T], func=AF.Ln, bias=one1),
                         reads=[("spt", i2), "cst"], writes=[("spt", i2)])
                    P.op("dve", lambda e, i2=i2: e.tensor_tensor_scan(out=ct[i2][:, 0:T], data0=rstT[:, 0:T], data1=spt[i2][:, 0:T],
                                                                      initial=0.0, op0=ALU.mult, op1=ALU.add),
                         reads=[("spt", i2), "cst"], writes=[("ct", i2)])
                    c3 = ct[i2][:, 0:T].rearrange("p (c t) -> p c t", t=64)
                    P.op("dve", lambda e, g=g, c3=c3: e.tensor_copy(out=cl[:, g, 0:nch], in_=c3[:, :, 63]),
                         reads=[("ct", i2)], writes=[("cl", g)])
                    P.op("dve", lambda e, g=g: e.tensor_scalar(out=nb[:, g, 0:nch], in0=cl[:, g, 0:nch], scalar1=-1.0 / 16, scalar2=None, op0=ALU.mult),
                         reads=[("cl", g)], writes=[("nb", g)])
                    P.op("act", lambda e, g=g: e.activation(out=el[:, g, 0:nch], in_=nb[:, g, 0:nch], func=AF.Exp),
                         reads=[("nb", g)], writes=[("el", g)])
                    if d == 1:
                        P.op("dve", lambda e, g=g: e.tensor_scalar(out=pbs[:, g, 0:nch], in0=cl[:, g, 0:nch], scalar1=1.0 / 16, scalar2=None, op0=ALU.mult),
                             reads=[("cl", g)], writes=[("pbs", g)])
                        P.op("dve", lambda e, g=g: e.tensor_scalar(out=nbq[:, g, 0:nch], in0=cl[:, g, 0:nch], scalar1=-1.0 / 16, scalar2=LNQ,
                                                                   op0=ALU.mult, op1=ALU.add),
                             reads=[("cl", g)], writes=[("nbq", g)])
                        P.op("dve", lambda e, i2=i2: e.tensor_tensor(out=ct[i2][:, 0:T], in0=ct[i2][:, 0:T], in1=spt[i2][:, 0:T], op=ALU.subtract),
                             reads=[("ct", i2), ("spt", i2)], writes=[("ct", i2)])
                    if d == 0:
                        if outputs:
                            P.op("act", lambda e, i2=i2: e.activation(out=Eq[:, 0:T], in_=ct[i2][:, 0:T], func=AF.Exp, scale=-1.0 / 16, bias=lnq),
                                 reads=[("ct", i2), "cst"], writes=["Eq"])
                            P.op("act", lambda e, i2=i2: e.activation(out=Ek[:, 0:T], in_=ct[i2][:, 0:T], func=AF.Exp, scale=1.0 / 16),
                                 reads=[("ct", i2)], writes=["Ek"])
                        for ch in range(nch):
                            P.op("act", lambda e, i2=i2, g=g, ch=ch: e.activation(
                                out=El[:, ch * 64:(ch + 1) * 64], in_=ct[i2][:, ch * 64:(ch + 1) * 64], func=AF.Exp, scale=1.0 / 16,
                                bias=nb[:, g, ch:ch + 1]), reads=[("ct", i2), ("nb", g)], writes=["El"])
                    else:
                        P.op("act", lambda e, i2=i2: e.activation(out=El[:, 0:T], in_=ct[i2][:, 0:T], func=AF.Exp, scale=-1.0 / 16),
                             reads=[("ct", i2)], writes=["El"])
                        if outputs:
                            for ch in range(nch):
                                P.op("act", lambda e, i2=i2, g=g, ch=ch: e.activation(
                                    out=Eq[:, ch * 64:(ch + 1) * 64], in_=ct[i2][:, ch * 64:(ch + 1) * 64], func=AF.Exp, scale=1.0 / 16,
                                    bias=nbq[:, g, ch:ch + 1]), reads=[("ct", i2), ("nbq", g)], writes=["Eq"])
                                P.op("act", lambda e, i2=i2, g=g, ch=ch: e.activation(
                                    out=Ek[:, ch * 64:(ch + 1) * 64], in_=ct[i2][:, ch * 64:(ch + 1) * 64], func=AF.Exp, scale=-1.0 / 16,
                                    bias=pbs[:, g, ch:ch + 1]), reads=[("ct", i2), ("pbs", g)], writes=["Ek"])
                    for which in (("k", "q") if outputs else ("k",)):
                        u = gkt if which == "k" else gqt
                        ut = "gkt" if which == "k" else "gqt"
                        P.op("pe", lambda e, u=u, g=g: e.matmul(pr[:, 0:T], lhsT=permb[:], rhs=u[:, g, 0:T], start=True, stop=True),
                             reads=["permb", ut], writes=["pr"])
                        P.op("pool", lambda e, u=u, g=g, a=a, i2=i2: e.tensor_tensor(out=t1[i2][:, 0:T], in0=u[:, g, 0:T], in1=ropet[:, a, 0:T], op=ALU.mult),
                             reads=[ut, "ropet"], writes=[("t1", i2)])
                        P.op("dve", lambda e, a=a, i2=i2: e.tensor_tensor(out=t2[i2][:, 0:T], in0=pr[:, 0:T], in1=ropet[:, 2 + a, 0:T], op=ALU.mult),
                             reads=["pr", "ropet"], writes=[("t2", i2)])
                        P.op("pool", lambda e, i2=i2: e.tensor_tensor(out=rk[i2][:, 0:T], in0=t1[i2][:, 0:T], in1=t2[i2][:, 0:T], op=ALU.add),
                             reads=[("t1", i2), ("t2", i2)], writes=[("rk", i2)])
                        if which == "k":
                            P.op("pool", lambda e, g=g, i2=i2: e.tensor_tensor(out=kl[:, g, 0:T], in0=rk[i2][:, 0:T], in1=El[:, 0:T], op=ALU.mult),
                                 reads=[("rk", i2), "El"], writes=[("kl", g)])
                            if outputs:
                                P.op("dve", lambda e, g=g, i2=i2: e.tensor_tensor(out=ke[:, g, 0:T], in0=rk[i2][:, 0:T], in1=Ek[:, 0:T], op=ALU.mult),
                                     reads=[("rk", i2), "Ek"], writes=[("ke", g)])
                        else:
                            P.op("dve", lambda e, g=g, i2=i2: e.tensor_tensor(out=qe[:, g, 0:T], in0=rk[i2][:, 0:T], in1=Eq[:, 0:T], op=ALU.mult),
                                 reads=[("rk", i2), "Eq"], writes=[("qe", g)])
                for ch in range(nch):
                    for half in range(2):
                        for j in range(4):
                            g = half * 4 + j
                            P.op("pe", lambda e, g=g, j=j, ch=ch: e.matmul(ptk[:, j, :], lhsT=kl[:, g, ch * 64:(ch + 1) * 64], rhs=identb[:],
                                                                        start=True, stop=True),
                                 reads=[("kl", g), "identb"], writes=["ptk"])
                        P.op("act", lambda e, ch=ch, half=half: e.activation(out=klT[:, ch, half * 4:(half + 1) * 4, :], in_=ptk[:], func=AF.Copy),
                             reads=["ptk"], writes=[("klT", ch)])
                for ch in chunks:
                    r0 = tok0 + ch * 64
                    vi = vcnt % 2
                    vcnt += 1
                    P.dma("sp", lambda e, vi=vi, r0=r0: e.dma_start(out=vt[vi][:], in_=src["bv"][r0:r0 + 64, :]), ("vt", vi), writes=[("vt", vi)])
                    if outputs:
                        for hd in range(4):
                            for a in range(2):
                                g = hd * 2 + a
                                P.op("pe", lambda e, hd=hd, g=g, a=a, ch=ch: e.matmul(
                                    pa[:, hd, :], lhsT=ke[:, g, ch * 64:(ch + 1) * 64], rhs=qe[:, g, ch * 64:(ch + 1) * 64],
                                    start=(a == 0), stop=(a == 1)), reads=[("ke", g), ("qe", g)], writes=["pa"])
                        P.op("dve", lambda e: e.tensor_tensor(out=attS[:], in0=pa[:], in1=mask, op=ALU.mult),
                             reads=["pa", "cst"], writes=["attS"])
                        for hd in range(4):
                            P.op("pe", lambda e, hd=hd, vi=vi: e.matmul(po[hd][:], lhsT=attS[:, hd, :], rhs=vt[vi][:, hd * 512:(hd + 1) * 512],
                                                                       start=True, stop=False),
                                 reads=["attS", ("vt", vi)], writes=[("po", hd)])
                            for a in range(2):
                                g = hd * 2 + a
                                P.op("pe", lambda e, hd=hd, g=g, a=a, ch=ch: e.matmul(
                                    po[hd][:], lhsT=qe[:, g, ch * 64:(ch + 1) * 64], rhs=Sbf[:, g, :], start=False, stop=(a == 1)),
                                    reads=[("qe", g), ("Sbf", g)], writes=[("po", hd)])
                        if d == 0:
                            for hd in range(4):
                                P.op("act", lambda e, hd=hd: e.activation(out=ost[:, hd * 512:(hd + 1) * 512], in_=po[hd][:], func=AF.Copy),
                                     reads=[("po", hd)], writes=["ost"])
                            P.dma("sp", lambda e, r0=r0: e.dma_start(out=src["of"][r0:r0 + 64, :], in_=ost[:]), "ost", reads=["ost"])
                        else:
                            P.dma("sp", lambda e, r0=r0: e.dma_start(out=oft[:], in_=src["of"][r0:r0 + 64, :]), "oft", writes=["oft"])
                            P.dma("sp", lambda e, r0=r0: e.dma_start(out=bzt[:], in_=src["bz"][r0:r0 + 64, :]), "bzt", writes=["bzt"])
                            P.op("act", lambda e: e.activation(out=sz[:], in_=bzt[:], func=AF.Silu), reads=["bzt"], writes=["sz"])
                            for hd in range(4):
                                hs = slice(hd * 512, (hd + 1) * 512)
                                P.op("dve", lambda e, hd=hd, hs=hs: e.tensor_tensor(out=ost[:, hs], in0=po[hd][:], in1=oft[:, hs], op=ALU.add),
                                     reads=[("po", hd), "oft"], writes=[("osth", hd)])
                                P.op("act", lambda e, hd=hd, hs=hs: e.activation(out=junk[:], in_=ost[:, hs], func=AF.Square,
                                                                                 accum_out=ssq[:, hd:hd + 1]),
                                     reads=[("osth", hd)], writes=["junk", ("ssq", hd)])
                            P.op("dve", lambda e: e.tensor_scalar(out=rr[:], in0=ssq[:], scalar1=1.0 / 512, scalar2=1e-6, op0=ALU.mult, op1=ALU.add),
                                 reads=[("ssq", h) for h in range(4)], writes=["rr"])
                            P.op("act", lambda e: e.activation(out=rr[:], in_=rr[:], func=AF.Sqrt), reads=["rr"], writes=["rr"])
                            P.op("dve", lambda e: e.reciprocal(out=rr[:], in_=rr[:]), reads=["rr"], writes=["rr"])
                            for hd in range(4):
                                hs = slice(hd * 512, (hd + 1) * 512)
                                P.op("dve", lambda e, hd=hd, hs=hs: e.scalar_tensor_tensor(
                                    out=ost[:, hs], in0=ost[:, hs], scalar=rr[:, hd:hd + 1], in1=ngt[:], op0=ALU.mult, op1=ALU.mult),
                                    reads=[("osth", hd), "rr", "ngt"], writes=[("osth", hd)])
                            P.op("pool", lambda e: e.tensor_tensor(out=zbt[:], in0=ost[:], in1=sz[:], op=ALU.mult),
                                 reads=[("osth", h) for h in range(4)] + ["sz"], writes=["zbt"])
                            P.dma("sp", lambda e, r0=r0: e.dma_start(out=src["zb"][r0:r0 + 64, :], in_=zbt[:]), "zbt", reads=["zbt"])
                    for g in range(8):
                        hd = g // 2
                        pu = pz if g % 2 == 0 else pr
                        put = "pz" if g % 2 == 0 else "pr"
                        P.op("pe", lambda e, g=g, hd=hd, pu=pu, ch=ch, vi=vi: e.matmul(
                            pu[:], lhsT=klT[:, ch, g, :], rhs=vt[vi][:, hd * 512:(hd + 1) * 512], start=True, stop=True),
                            reads=[("klT", ch), ("vt", vi)], writes=[put])
                        P.op("dve", lambda e, g=g, pu=pu, ch=ch: e.scalar_tensor_tensor(
                            out=S[:, g, :], in0=S[:, g, :], scalar=el[:, g, ch:ch + 1], in1=pu[:], op0=ALU.mult, op1=ALU.add),
                            reads=[("S", g), ("el", g), put], writes=[("S", g)])
                        if outputs:
                            P.op("pool", lambda e, g=g: e.tensor_copy(out=Sbf[:, g, :], in_=S[:, g, :]),
                                 reads=[("S", g)], writes=[("Sbf", g)])
            if s_mix is not None:
                ap2, m1, m2 = s_mix
                Smx = sb(es, "Smx", [128, 8, 512], F32)
                allS = [("S", g) for g in range(8)]
                P.dma("sp", lambda e: e.dma_start(out=Smx[:].rearrange("p g n -> p (g n)"), in_=ap2), "Smx", writes=["Smx"])
                P.op("dve", lambda e: e.tensor_scalar(out=S[:], in0=S[:], scalar1=m1, scalar2=None, op0=ALU.mult),
                     reads=allS + ["cst"], writes=allS)
                P.op("dve", lambda e: e.scalar_tensor_tensor(out=S[:], in0=Smx[:], scalar=m2, in1=S[:], op0=ALU.mult, op1=ALU.add),
                     reads=allS + ["Smx", "cst"], writes=allS)
            if s_final is not None:
                P.dma("sp", lambda e: e.dma_start(out=s_final, in_=S[:].rearrange("p g n -> p (g n)")), "Sst",
                      reads=[("S", g) for g in range(8)])

        csrc = dict(gk=c_gk, lr=c_lr, bv=c_bv, rope=ropec)
        msrc = dict(gk=s_gk, gq=s_gq, lr=s_lr, bv=s_bv, bz=s_bz, rope=ropem, of=s_of, zb=s_zb)
        for d in range(2):
            with contextlib.ExitStack() as es:
                P = newprog()
                gla_phase(es, P, csrc, 256, 1, d, False, None, s_sctx[d])
                P.emit()
        if stop_after <= 3:
            return nc
        for d in range(2):
            with contextlib.ExitStack() as es:
                P = newprog(pe_nowait=True)
                proj_phase(es, P, xp[d], 4096, 0,
                           fm_specs=[(C_BK, 2, p_gk, 0, 0, 1.0)],
                           tm_specs=[(C_BV, 4, p_bv, 0, 0, "plain")],
                           lr_dst=p_lr)
                P.emit()
            with contextlib.ExitStack() as es:
                P = newprog()
                psrc = dict(gk=p_gk, lr=p_lr, bv=p_bv, rope=ropep[d])
                gla_phase(es, P, psrc, 512, 8, d, False, (s_sctx[d], fA[d]), s_pre[d], s_mix=(s_sctx[d], f1[d], f2[d]))
                P.emit()
        for d in range(2):
            with contextlib.ExitStack() as es:
                P = newprog()
                gla_phase(es, P, msrc, 512, 8, d, True, (s_pre[d], one1), None)
                P.emit()
        if stop_after <= 4:
            return nc

        with contextlib.ExitStack() as es:
            P = newprog()
            bt2b = sb(es, "bt2b", [64, 16 * 8 * 128], BF16)
            mskb = sb(es, "mskb", [1, 8 * 6 * 128], BF16)
            onesb = sb(es, "onesb", [1, 64], BF16)
            qTt = sb(es, "qTt", [128, 8, 512], BF16)
            kband = sb(es, "kband", [128, 8, 960], BF16)
            vband = sb(es, "vband", [128, 14, 1040], BF16)
            PT = [sb(es, "PT%d" % i, [128, 512], BF16) for i in range(2)]
            azt = sb(es, "azt", [64, 1024], BF16)
            sza = sb(es, "sza", [64, 1024], F32)
            ya = sb(es, "ya", [64, 1024], F32)
            rcp = sb(es, "rcp", [64, 16], F32)
            zat = sb(es, "zat", [64, 1024], BF16)
            ST = [ps(es, "ST%d" % i, [128, 512], F32) for i in range(2)]
            OA = ps(es, "OA", [64, 3, 512], F32)
            P.dma("pool", lambda e: e.dma_start(out=bt2b[:], in_=bt2_d), "bt2b", writes=["bt2b"])
            P.dma("pool", lambda e: e.dma_start(out=mskb[:], in_=msk_d), "mskb", writes=["mskb"])
            P.op("dve", lambda e: e.memset(onesb[:], 1.0), writes=["onesb"])
            P.dma("sp", lambda e: e.dma_start(out=kTc[:], in_=c_nak.rearrange("m p t -> p m t")), "kTc", writes=["kTc"])
            P.dma("sp", lambda e: e.dma_start(out=vc[:], in_=c_nav.rearrange("(b p) n -> p b n", p=128)), "vc", writes=["vc"])
            bt2v = bt2b[:].rearrange("c (h q k) -> c h q k", h=16, q=8)
            mskv = mskb[:].rearrange("o (s j k) -> o s j k", s=8, j=6)
            scnt = 0
            for gi in range(8):
                e0 = gi * 8 * 64
                P.dma("sp", lambda e, gi=gi: e.dma_start(out=qTt[:], in_=s_naq[:, :, gi * 512:(gi + 1) * 512].rearrange("m p t -> p m t")),
                      "qTt", writes=["qTt"])
                P.dma("sp", lambda e, e0=e0: e.dma_start(out=kband[:], in_=s_nak[:, :, e0:e0 + 960].rearrange("m p t -> p m t")),
                      "kband", writes=["kband"])
                for bi in range(14):
                    P.dma("sp", lambda e, bi=bi, e0=e0: e.dma_start(out=vband[:, bi, :], in_=s_nav[e0 + bi * 64:e0 + bi * 64 + 128, :]),
                          ("vband", bi), writes=[("vband", bi)])
                for ri in range(8):
                    lr = gi * 8 + ri
                    if lr < 4:
                        pairs, si = list(range(2, 8)), lr
                    elif lr >= 60:
                        pairs, si = list(range(0, 6)), lr - 56
                    else:
                        pairs, si = list(range(2, 6)), None
                    npair = len(pairs)
                    nblk = npair + 2
                    P.dma("sp", lambda e, lr=lr: e.dma_start(out=azt[:], in_=s_az[lr * 64:(lr + 1) * 64, :]), "azt", writes=["azt"])
                    P.op("act", lambda e: e.activation(out=sza[:], in_=azt[:], func=AF.Silu), reads=["azt"], writes=["sza"])
                    for h in range(16):
                        m, pb = h // 2, 64 * (h % 2)
                        sti = scnt % 2
                        scnt += 1
                        qv = qTt[pb:pb + 64, m, ri * 64:(ri + 1) * 64]
                        for j, pi in enumerate(pairs):
                            bi = lr + 2 * pi - 8 - (8 * gi - 4)
                            off = bi * 64
                            P.op("pe", lambda e, sti=sti, j=j, m=m, pb=pb, off=off, qv=qv: e.matmul(
                                ST[sti][:, j * 64:(j + 1) * 64], lhsT=kband[pb:pb + 64, m, off:off + 128], rhs=qv, start=True, stop=False),
                                reads=["kband", "qTt"], writes=[("ST", sti)])
                            P.op("pe", lambda e, sti=sti, j=j, h=h, pi=pi: e.matmul(
                                ST[sti][:, j * 64:(j + 1) * 64], lhsT=bt2v[:, h, pi, :], rhs=identb[0:64, 0:64], start=False, stop=(si is None)),
                                reads=["bt2b", "identb"], writes=[("ST", sti)])
                            if si is not None:
                                P.op("pe", lambda e, sti=sti, j=j, si=si: e.matmul(
                                    ST[sti][:, j * 64:(j + 1) * 64], lhsT=mskv[:, si, j, :], rhs=onesb[:], start=False, stop=True),
                                    reads=["mskb", "onesb"], writes=[("ST", sti)])
                        for cb in range(2):
                            j = npair + cb
                            P.op("pe", lambda e, sti=sti, j=j, m=m, pb=pb, cb=cb, qv=qv: e.matmul(
                                ST[sti][:, j * 64:(j + 1) * 64], lhsT=kTc[pb:pb + 64, m, cb * 128:(cb + 1) * 128], rhs=qv, start=True, stop=True),
                                reads=["kTc", "qTt"], writes=[("ST", sti)])
                        P.op("act", lambda e, sti=sti, nblk=nblk: e.activation(out=PT[sti][:, 0:nblk * 64], in_=ST[sti][:, 0:nblk * 64], func=AF.Exp),
                             reads=[("ST", sti)], writes=[("PT", sti)])
                        ob, oo = h % 3, (h // 3) * 65
                        for j in range(nblk):
                            if j < npair:
                                bi = lr + 2 * pairs[j] - 8 - (8 * gi - 4)
                                rhs = vband[:, bi, h * 65:(h + 1) * 65]
                                rd = ("vband", bi)
                            else:
                                rhs = vc[:, j - npair, h * 65:(h + 1) * 65]
                                rd = "vc"
                            P.op("pe", lambda e, sti=sti, j=j, rhs=rhs, ob=ob, oo=oo, nblk=nblk: e.matmul(
                                OA[:, ob, oo:oo + 65], lhsT=PT[sti][:, j * 64:(j + 1) * 64], rhs=rhs, start=(j == 0), stop=(j == nblk - 1)),
                                reads=[("PT", sti), rd], writes=[("OA", ob)])
                        P.op("dve", lambda e, h=h, ob=ob, oo=oo: e.reciprocal(out=rcp[:, h:h + 1], in_=OA[:, ob, oo + 64:oo + 65]),
                             reads=[("OA", ob)], writes=[("rcp", h)])
                        P.op("dve", lambda e, h=h, ob=ob, oo=oo: e.tensor_scalar(
                            out=ya[:, h * 64:(h + 1) * 64], in0=OA[:, ob, oo:oo + 64], scalar1=rcp[:, h:h + 1], scalar2=None, op0=ALU.mult),
                            reads=[("OA", ob), ("rcp", h)], writes=[("ya", h)])
                    P.op("dve", lambda e: e.tensor_tensor(out=zat[:], in0=ya[:], in1=sza[:], op=ALU.mult),
                         reads=[("ya", h) for h in range(16)] + ["sza"], writes=["zat"])
                    P.dma("sp", lambda e, lr=lr: e.dma_start(out=s_za[lr * 64:(lr + 1) * 64, :], in_=zat[:]), "zat", reads=["zat"])
            P.emit()
        if stop_after <= 5:
            return nc

        with contextlib.ExitStack() as es:
            P = newprog()
            zin_a = [sb(es, "zin_a%d" % i, [128, 1024], BF16) for i in range(1)]
            zin_b = [sb(es, "zin_b%d" % i, [128, 2048], BF16) for i in range(1)]
            zaT = sb(es, "zaT", [128, 8, 512], BF16)
            zbT = sb(es, "zbT", [128, 16, 512], BF16)
            mgt = [sb(es, "mgt%d" % i, [128, 8, 512], BF16) for i in range(2)]
            mT = sb(es, "mT", [128, 16, 512], BF16)
            wa = sb(es, "wa", [128, 8, 512], BF16)
            wb = sb(es, "wb", [128, 16, 512], BF16)
            wo = sb(es, "wo", [128, 16, 512], BF16)
            sa = sb(es, "sa", [128, 512], F32)
            sbb = sb(es, "sbb", [128, 512], F32)
            m1 = sb(es, "m1", [128, 512], F32)
            m2 = sb(es, "m2", [128, 512], F32)
            xr = sb(es, "xr", [128, 2048], F32)
            ut = sb(es, "ut", [128, 4, 2048], F32)
            yo = sb(es, "yo", [128, 2048], F32)
            lng = sb(es, "lng", [128, 4096], F32)
            st = sb(es, "mst", [128, 4, 6], F32)
            mv = sb(es, "mmv", [128, 2], F32)
            rs = sb(es, "mrs", [128, 2], F32)
            tpz = [ps(es, "tpz%d" % i, [128, 4, 128], F32) for i in range(2)]
            ppa = ps(es, "ppa", [128, 512], F32)
            ppb = ps(es, "ppb", [128, 512], F32)
            pout = [ps(es, "pout%d" % i, [128, 512], F32) for i in range(2)]
            P.dma("sp", lambda e: e.dma_start(out=lng[:], in_=lngb), "lng", writes=["lng"])
            tcnt = 0
            ocnt = 0
            for t in range(8):
                tok0 = t * 512
                for s in range(4):
                    zi = 0
                    r0 = tok0 + s * 128
                    P.dma("sp", lambda e, zi=zi, r0=r0: e.dma_start(out=zin_a[zi][:], in_=s_za[r0:r0 + 128, :]), ("zin_a", zi), writes=[("zin_a", zi)])
                    P.dma("sp", lambda e, zi=zi, r0=r0: e.dma_start(out=zin_b[zi][:], in_=s_zb[r0:r0 + 128, :]), ("zin_b", zi), writes=[("zin_b", zi)])
                    for (zin, nm, nk, dstT, dn) in ((zin_a, "zin_a", 8, zaT, "zaT"), (zin_b, "zin_b", 16, zbT, "zbT")):
                        for kg in range(nk // 4):
                            ti = tcnt % 2
                            tcnt += 1
                            for kk in range(4):
                                k = kg * 4 + kk
                                P.op("pe", lambda e, zin=zin, zi=zi, k=k, kk=kk, ti=ti: e.matmul(
                                    tpz[ti][:, kk, :], lhsT=zin[zi][:, k * 128:(k + 1) * 128], rhs=identb[:], start=True, stop=True),
                                    reads=[(nm, zi), "identb"], writes=[("tpz", ti)])
                            if tcnt % 2 == 0:
                                P.op("act", lambda e, dstT=dstT, kg=kg, s=s, ti=ti: e.activation(
                                    out=dstT[:, kg * 4:(kg + 1) * 4, s * 128:(s + 1) * 128], in_=tpz[ti][:], func=AF.Copy),
                                    reads=[("tpz", ti)], writes=[(dn, s)])
                            else:
                                P.op("dve", lambda e, dstT=dstT, kg=kg, s=s, ti=ti: e.tensor_copy(
                                    out=dstT[:, kg * 4:(kg + 1) * 4, s * 128:(s + 1) * 128], in_=tpz[ti][:]),
                                    reads=[("tpz", ti)], writes=[(dn, s)])
                zaTr = [("zaT", s) for s in range(4)]
                zbTr = [("zbT", s) for s in range(4)]
                for fg in range(4):
                    c0 = fg * 512
                    gi2 = fg % 2
                    P.dma("pool", lambda e, c0=c0: e.dma_start(out=wa[:], in_=wbra[:, c0:c0 + 512].rearrange("(k p) n -> p k n", p=128)),
                          "wa", writes=["wa"])
                    P.dma("pool", lambda e, c0=c0: e.dma_start(out=wb[:], in_=wbrb[:, c0:c0 + 512].rearrange("(k p) n -> p k n", p=128)),
                          "wb", writes=["wb"])
                    P.dma("sp", lambda e, fg=fg, gi2=gi2, tok0=tok0: e.dma_start(
                        out=mgt[gi2][:, 0:4, :], in_=s_mg[fg * 4:fg * 4 + 4, :, tok0:tok0 + 512].rearrange("c p t -> p c t")),
                        ("mgta", gi2), writes=[("mgta", gi2)])
                    P.dma("sp", lambda e, fg=fg, gi2=gi2, tok0=tok0: e.dma_start(
                        out=mgt[gi2][:, 4:8, :], in_=s_mg[16 + fg * 4:16 + fg * 4 + 4, :, tok0:tok0 + 512].rearrange("c p t -> p c t")),
                        ("mgtb", gi2), writes=[("mgtb", gi2)])
                    for n in range(4):
                        fc = fg * 4 + n
                        for k in range(8):
                            P.op("pe", lambda e, k=k, n=n: e.matmul(ppa[:], lhsT=wa[:, k, n * 128:(n + 1) * 128], rhs=zaT[:, k, :],
                                                                   start=(k == 0), stop=(k == 7)),
                                 reads=["wa"] + zaTr, writes=["ppa"])
                        for k in range(16):
                            P.op("pe", lambda e, k=k, n=n: e.matmul(ppb[:], lhsT=wb[:, k, n * 128:(n + 1) * 128], rhs=zbT[:, k, :],
                                                                   start=(k == 0), stop=(k == 15)),
                                 reads=["wb"] + zbTr, writes=["ppb"])
                        P.op("act", lambda e, gi2=gi2, n=n: e.activation(out=sa[:], in_=mgt[gi2][:, n, :], func=AF.Sigmoid),
                             reads=[("mgta", gi2)], writes=["sa"])
                        P.op("act", lambda e, gi2=gi2, n=n: e.activation(out=sbb[:], in_=mgt[gi2][:, 4 + n, :], func=AF.Sigmoid),
                             reads=[("mgtb", gi2)], writes=["sbb"])
                        P.op("dve", lambda e: e.tensor_tensor(out=m1[:], in0=ppa[:], in1=sa[:], op=ALU.mult), reads=["ppa", "sa"], writes=["m1"])
                        P.op("dve", lambda e: e.tensor_tensor(out=m2[:], in0=ppb[:], in1=sbb[:], op=ALU.mult), reads=["ppb", "sbb"], writes=["m2"])
                        P.op("dve", lambda e, fc=fc: e.tensor_tensor(out=mT[:, fc, :], in0=m1[:], in1=m2[:], op=ALU.add),
                             reads=["m1", "m2"], writes=[("mT", fc)])
                mTr = [("mT", fc) for fc in range(16)]
                for ng in range(4):
                    c0 = ng * 512
                    P.dma("pool", lambda e, c0=c0: e.dma_start(out=wo[:], in_=wout[:, c0:c0 + 512].rearrange("(k p) n -> p k n", p=128)),
                          "wo", writes=["wo"])
                    for s in range(4):
                        oi = ocnt % 2
                        ocnt += 1
                        for k in range(16):
                            P.op("pe", lambda e, k=k, s=s, oi=oi: e.matmul(pout[oi][:], lhsT=mT[:, k, s * 128:(s + 1) * 128], rhs=wo[:, k, :],
                                                                          start=(k == 0), stop=(k == 15)),
                                 reads=["wo"] + mTr, writes=[("pout", oi)])
                        P.op("dve", lambda e, s=s, oi=oi, c0=c0: e.tensor_tensor(out=ut[:, s, c0:c0 + 512], in0=pout[oi][:], in1=gateb[:, c0:c0 + 512], op=ALU.mult),
                             reads=[("pout", oi), "gateb"], writes=[("ut", s)])
                for s in range(4):
                    r0 = tok0 + s * 128
                    P.dma("sp", lambda e, r0=r0: e.dma_start(out=xr[:], in_=xm[r0:r0 + 128, :]), "xr", writes=["xr"])
                    P.op("dve", lambda e, s=s: e.scalar_tensor_tensor(out=ut[:, s, :], in0=xr[:], scalar=float(2.0 ** 0.25), in1=ut[:, s, :],
                                                                      op0=ALU.mult, op1=ALU.add),
                         reads=["xr", ("ut", s)], writes=[("ut", s)])
                    for j in range(4):
                        P.op("dve", lambda e, s=s, j=j: e.bn_stats(out=st[:, j, :], in_=ut[:, s, j * 512:(j + 1) * 512]),
                             reads=[("ut", s)], writes=["mst"])
                    P.op("dve", lambda e: e.bn_aggr(out=mv[:], in_=st[:].rearrange("p a b -> p (a b)")), reads=["mst"], writes=["mmv"])
                    P.op("dve", lambda e: e.tensor_scalar(out=rs[:, 0:1], in0=mv[:, 1:2], scalar1=1e-6, scalar2=None, op0=ALU.add),
                         reads=["mmv"], writes=["mrs0"])
                    P.op("act", lambda e: e.activation(out=rs[:, 0:1], in_=rs[:, 0:1], func=AF.Sqrt), reads=["mrs0"], writes=["mrs0"])
                    P.op("dve", lambda e: e.reciprocal(out=rs[:, 1:2], in_=rs[:, 0:1]), reads=["mrs0"], writes=["mrs1"])
                    P.op("dve", lambda e, s=s: e.tensor_scalar(out=yo[:], in0=ut[:, s, :], scalar1=mv[:, 0:1], scalar2=rs[:, 1:2],
                                                               op0=ALU.subtract, op1=ALU.mult),
                         reads=[("ut", s), "mmv", "mrs1"], writes=["yo"])
                    P.op("dve", lambda e: e.tensor_tensor(out=yo[:], in0=yo[:], in1=lng[:, 0:2048], op=ALU.mult), reads=["yo", "lng"], writes=["yo"])
                    P.op("dve", lambda e: e.tensor_tensor(out=yo[:], in0=yo[:], in1=lng[:, 2048:4096], op=ALU.add), reads=["yo", "lng"], writes=["yo"])
                    P.dma("sp", lambda e, r0=r0: e.dma_start(out=yout[r0:r0 + 128, :], in_=yo[:]), "yo", reads=["yo"])
            P.emit()
    return nc


_NC_CACHE = {}


def _consts():
    cst = np.zeros((128, 2048), np.float32)
    cst[:, 0:128] = np.eye(128, dtype=np.float32)
    perm = np.zeros((128, 128), np.float32)
    for m in range(128):
        perm[(m + 64) % 128, m] = 1.0
    cst[:, 128:256] = perm
    j = np.arange(64)[:, None]
    i = np.arange(64)[None, :]
    cst[0:64, 256:512] = np.tile((j <= i).astype(np.float32), (1, 4))
    cst[0:64, 512:768] = np.tile((j >= i).astype(np.float32), (1, 4))
    r = np.ones((128, 512), np.float32)
    r[:, 0::64] = 0.0
    cst[:, 768:1280] = r
    cst[:, 1282] = LNQ
    cst[:, 1283] = 0.0
    cst[:, 1284] = 1.0
    cst[:, 1285] = 1e-6
    return cst


def _rope_tables(rows, cols):
    inv = (np.float32(10000.0) ** (-np.arange(64, dtype=np.float32) / np.float32(64))).astype(np.float32)
    out = np.zeros((128, 4, rows.shape[0]), np.float32)
    for a, pos in enumerate((rows, cols)):
        ang = (pos[None, :].astype(np.float32) * inv[:, None]).astype(np.float32)
        c = np.cos(ang).astype(np.float32)
        s = np.sin(ang).astype(np.float32)
        out[0:64, a] = c
        out[64:128, a] = c
        out[0:64, 2 + a] = -s
        out[64:128, 2 + a] = s
    return out


def _fm(v):
    return np.ascontiguousarray(v.reshape(16, 128).T)


def _host_prep(x, c, ctx, c_ctx, w_mod, b_mod, w_in, na_rpb, gla_w_gate2, gla_b_gate,
               gla_norm_g, w_br_a, w_br_b, w_out, ln_g, ln_b):
    f32 = np.float32
    x = np.asarray(x, f32); c = np.asarray(c, f32); ctx = np.asarray(ctx, f32); c_ctx = np.asarray(c_ctx, f32)
    w_mod = np.asarray(w_mod, f32)[0]; b_mod = np.asarray(b_mod, f32)[0]; w_in = np.asarray(w_in, f32)[0]
    rpb = np.asarray(na_rpb, f32)[0]; wg2 = np.asarray(gla_w_gate2, f32)[0]; bgt = np.asarray(gla_b_gate, f32)[0]
    ng = np.asarray(gla_norm_g, f32)[0]
    wbra = np.asarray(w_br_a, f32)[0]; wbrb = np.asarray(w_br_b, f32)[0]; wout = np.asarray(w_out, f32)[0]
    lng = np.asarray(ln_g, f32)[0]; lnb = np.asarray(ln_b, f32)[0]
    bmodT = np.concatenate([_fm(b_mod[0:2048]), _fm(b_mod[2048:4096])], axis=1)
    bgate = np.ascontiguousarray(np.broadcast_to(b_mod[4096:6144][None, :], (128, 2048)))
    w2d = np.zeros((17, 2, 8, 128), f32)
    for d in range(2):
        for hd in range(4):
            for a in range(2):
                cols = hd * 128 + a * 64 + np.concatenate([np.arange(64), np.arange(64)])
                w2d[0:16, d, hd * 2 + a, :] = wg2[d][:, cols]
                w2d[16, d, hd * 2 + a, :] = bgt[d][cols]
    w2d = w2d.reshape(17, 2048)
    cc = np.arange(64)
    cstart = np.clip(cc - 8, 0, 48)
    jj = np.arange(64)
    inwin = (jj[None, :] >= cstart[:, None]) & (jj[None, :] < cstart[:, None] + 16)
    dc = np.clip(jj[None, :] - cc[:, None] + 15, 0, 30)
    bt2 = np.full((64, 16, 8, 2, 64), NEG, f32)
    for pi in range(8):
        for w in range(2):
            dr = 2 * pi - 1 + w
            if dr < 0 or dr > 14:
                continue
            vals = rpb[:, dr, :][:, dc]
            vals = np.where(inwin[None], vals, f32(NEG))
            bt2[:, :, pi, w, :] = vals.transpose(1, 0, 2)
    bt2 = bt2.reshape(64, 16 * 8 * 128)
    ngrep = np.ascontiguousarray(np.broadcast_to(ng[None, :], (64, 512)))
    lngb = np.ascontiguousarray(np.concatenate([np.broadcast_to(lng[None, :], (128, 2048)),
                                                np.broadcast_to(lnb[None, :], (128, 2048))], axis=1))
    ropec = _rope_tables(np.zeros(256, f32), np.zeros(256, f32))[None]
    cst0 = _consts()
    in_maps = []
    for core in range(8):
        b, q = core // 4, core % 4
        xb = x[b]
        xm = np.ascontiguousarray(xb[q * 4096:(q + 1) * 4096])
        xh = np.zeros((512, 2048), f32)
        if q > 0:
            xh[0:256] = xb[q * 4096 - 256:q * 4096]
        if q < 3:
            xh[256:512] = xb[(q + 1) * 4096:(q + 1) * 4096 + 256]
        cT = np.zeros((128, 16, 2), f32)
        cT[:, :, 0] = _fm(c[b])
        cT[:, :, 1] = _fm(c_ctx)
        pos = np.arange(q * 4096, (q + 1) * 4096)
        rt = _rope_tables((pos // 64).astype(f32), (pos % 64).astype(f32))
        ropem = np.ascontiguousarray(rt.reshape(128, 4, 8, 512).transpose(2, 0, 1, 3))
        msk = np.zeros((8, 6, 2, 64), f32)
        for si in range(8):
            lr = si if si < 4 else 56 + si
            pairs = list(range(2, 8)) if si < 4 else list(range(0, 6))
            r = 64 * q + lr
            rstart = min(max(r - 4, 0), 248)
            for j, pi in enumerate(pairs):
                for w in range(2):
                    rr = r + (2 * pi - 1 + w) - 7
                    ok = (rstart <= rr < rstart + 8)
                    msk[si, j, w, :] = 0.0 if ok else NEG
        cst = cst0.copy()
        cst[:, 1280] = 1.0 if q == 0 else 0.0
        cst[:, 1281] = 1.0 if q == 3 else 0.0
        cst[:, 1286] = 1.0 if q == 1 else 0.0
        cst[:, 1287] = 1.0 if q >= 1 else 0.0
        cst[:, 1288] = 1.0 if q == 0 else 0.0
        cst[:, 1289] = 1.0 if q == 2 else 0.0
        cst[:, 1290] = 1.0 if q <= 2 else 0.0
        cst[:, 1291] = 1.0 if q == 3 else 0.0
        xpn = np.zeros((2, 4096, 2048), f32)
        ropep = np.zeros((2, 8, 128, 4, 512), f32)
        for dd, qq in ((0, q - 1), (1, q + 1)):
            qc = min(max(qq, 0), 3)
            pp = np.arange(qc * 4096, (qc + 1) * 4096)
            rtp = _rope_tables((pp // 64).astype(f32), (pp % 64).astype(f32))
            ropep[dd] = rtp.reshape(128, 4, 8, 512).transpose(2, 0, 1, 3)
            if 0 <= qq <= 3:
                xpn[dd] = xb[qq * 4096:(qq + 1) * 4096]
        in_maps.append(dict(
            xm=xm, xh=xh, xc=np.ascontiguousarray(ctx[b]), cT=cT.reshape(128, 32), wmod=w_mod, bmodT=bmodT, bgate=bgate,
            win=w_in, ropem=ropem, ropec=ropec, xp=xpn, ropep=ropep, w2d=w2d, bt2=bt2, msk=msk.reshape(1, -1), cst=cst, ngrep=ngrep,
            wbra=wbra, wbrb=wbrb, wout=wout, lngb=lngb))
    return in_maps


def kernel(**inputs):
    in_maps = _host_prep(**inputs)
    if "nc" not in _NC_CACHE:
        _NC_CACHE["nc"] = build()
    nc = _NC_CACHE["nc"]
    res = run_bass_kernel_spmd(nc, in_maps, core_ids=list(range(8)))
    out = np.zeros((2, 16384, 2048), np.float32)
    for core in range(8):
        b, q = core // 4, core % 4
        out[b, q * 4096:(q + 1) * 4096] = res.results[core]["yout"]
    return out
```
